# Optimizing a Trainium2 kernel written in Bass

```python
import math
import jax, jax.numpy as jnp
from jax import lax
import numpy as np


D_MODEL = 1024
BATCH = 8
SEQ = 4096
DEPTH = 2

N_A = DEPTH // 2
N_B = DEPTH - N_A

DIFF_HEADS = 8
DIFF_HEAD_DIM = 64
DIFF_V_DIM = 2 * DIFF_HEAD_DIM
DIFF_QK_WIDTH = DIFF_HEADS * 2 * DIFF_HEAD_DIM
DIFF_V_WIDTH = DIFF_HEADS * DIFF_V_DIM

MLA_HEADS = 8
MLA_NOPE = 128
MLA_ROPE = 64
MLA_V = 128
Q_LORA = 384
KV_LORA = 256

D_FF = 4 * D_MODEL

ROPE_THETA = 10000.0
EPS = 1e-6
SUBLN_EPS = 1e-5
Q_BLOCK = 128

kernel_name = "yoco_diffattn_mla_hybrid"


def rms_norm(x, g, eps=EPS):
    xf = x.astype(jnp.float32)
    y = xf * lax.rsqrt(jnp.mean(xf * xf, axis=-1, keepdims=True) + eps)
    return (y * g.astype(jnp.float32)).astype(x.dtype)


def rope_tables(seq, dim):
    pos = jnp.arange(seq, dtype=jnp.float32)
    inv_freq = ROPE_THETA ** (-jnp.arange(0, dim, 2, dtype=jnp.float32) / dim)
    ang = pos[:, None] * inv_freq[None, :]
    return jnp.cos(ang), jnp.sin(ang)


def apply_rope(x, cos, sin):
    half = x.shape[-1] // 2
    x1, x2 = x[..., :half], x[..., half:]
    c = cos[None, :, None, :].astype(x.dtype)
    s = sin[None, :, None, :].astype(x.dtype)
    return jnp.concatenate([x1 * c - x2 * s, x2 * c + x1 * s], axis=-1)


def causal_mask(i, seq):
    q_pos = i * Q_BLOCK + jnp.arange(Q_BLOCK)
    k_pos = jnp.arange(seq)
    return k_pos[None, :] <= q_pos[:, None]


def diff_attention(hn, w_qkv, w_o, lq1, lk1, lq2, lk2, subln_g, lambda_init, cos, sin):
    B, S, _ = hn.shape
    H, d = DIFF_HEADS, DIFF_HEAD_DIM
    qkv = hn @ w_qkv
    q, k, v = jnp.split(qkv, [DIFF_QK_WIDTH, 2 * DIFF_QK_WIDTH], axis=-1)
    q = apply_rope(q.reshape(B, S, 2 * H, d), cos, sin).reshape(B, S, H, 2, d)
    k = apply_rope(k.reshape(B, S, 2 * H, d), cos, sin).reshape(B, S, H, 2, d)
    v = v.reshape(B, S, H, DIFF_V_DIM)
    lam = (jnp.exp(jnp.sum(lq1.astype(jnp.float32) * lk1.astype(jnp.float32)))
           - jnp.exp(jnp.sum(lq2.astype(jnp.float32) * lk2.astype(jnp.float32)))
           + lambda_init)
    scale = d ** -0.5
    nb = S // Q_BLOCK
    qb = q.reshape(B, nb, Q_BLOCK, H, 2, d).transpose(1, 0, 2, 3, 4, 5)

    def block(args):
        q_blk, i = args
        s = jnp.einsum('bqhcd,bkhcd->bhcqk', q_blk, k).astype(jnp.float32) * scale
        s = jnp.where(causal_mask(i, S)[None, None, None], s, -jnp.inf)
        p = jax.nn.softmax(s, axis=-1)
        a = p[:, :, 0] - lam * p[:, :, 1]
        return jnp.einsum('bhqk,bkhe->bqhe', a.astype(v.dtype), v)

    o = lax.map(block, (qb, jnp.arange(nb)))
    o = o.transpose(1, 0, 2, 3, 4).reshape(B, S, H, DIFF_V_DIM)
    o = rms_norm(o, subln_g, SUBLN_EPS) * (1.0 - lambda_init)
    return o.reshape(B, S, DIFF_V_WIDTH) @ w_o


def mla_shared_kv(h, kv_in_norm_g, w_dkv, kv_norm_g, w_ukv, cos, sin):
    B, S, _ = h.shape
    hn = rms_norm(h, kv_in_norm_g)
    ckv = hn @ w_dkv
    c, k_rope = ckv[..., :KV_LORA], ckv[..., KV_LORA:]
    c = rms_norm(c, kv_norm_g)
    kv = (c @ w_ukv).reshape(B, S, MLA_HEADS, MLA_NOPE + MLA_V)
    k_nope, v = kv[..., :MLA_NOPE], kv[..., MLA_NOPE:]
    k_rope = apply_rope(k_rope[:, :, None, :], cos, sin)[:, :, 0, :]
    return k_nope, k_rope, v


def mla_attention(hn, w_dq, q_norm_g, w_uq, w_o, k_nope, k_rope, v, cos, sin):
    B, S, _ = hn.shape
    H = MLA_HEADS
    cq = rms_norm(hn @ w_dq, q_norm_g)
    q = (cq @ w_uq).reshape(B, S, H, MLA_NOPE + MLA_ROPE)
    q_nope = q[..., :MLA_NOPE]
    q_rope = apply_rope(q[..., MLA_NOPE:], cos, sin)
    scale = (MLA_NOPE + MLA_ROPE) ** -0.5
    nb = S // Q_BLOCK
    qn_b = q_nope.reshape(B, nb, Q_BLOCK, H, MLA_NOPE).transpose(1, 0, 2, 3, 4)
    qr_b = q_rope.reshape(B, nb, Q_BLOCK, H, MLA_ROPE).transpose(1, 0, 2, 3, 4)

    def block(args):
        qn, qr, i = args
        s = (jnp.einsum('bqhd,bkhd->bhqk', qn, k_nope)
             + jnp.einsum('bqhr,bkr->bhqk', qr, k_rope)).astype(jnp.float32) * scale
        s = jnp.where(causal_mask(i, S)[None, None], s, -jnp.inf)
        p = jax.nn.softmax(s, axis=-1)
        return jnp.einsum('bhqk,bkhe->bqhe', p.astype(v.dtype), v)

    o = lax.map(block, (qn_b, qr_b, jnp.arange(nb)))
    o = o.transpose(1, 0, 2, 3, 4).reshape(B, S, H * MLA_V)
    return o @ w_o


def sq_relu_mlp(hn, w_up, w_down):
    return jnp.square(jax.nn.relu(hn @ w_up)) @ w_down


def _w(key, shape, fan_in):
    return jax.random.normal(key, shape, jnp.float32) * fan_in ** -0.5


def _g(key, shape):
    return 1.0 + 0.02 * jax.random.normal(key, shape, jnp.float32)


def setup_inputs(seed: int = 0) -> dict:
    key = jax.random.key(seed)
    ks = jax.random.split(key, 24)
    return {
        "x": jax.random.normal(ks[0], (BATCH, SEQ, D_MODEL), jnp.float32),
        "attn_norm_g": _g(ks[1], (DEPTH, D_MODEL)),
        "w_qkv_a": _w(ks[2], (N_A, D_MODEL, 2 * DIFF_QK_WIDTH + DIFF_V_WIDTH), D_MODEL),
        "lambda_q1": 0.1 * jax.random.normal(ks[3], (N_A, DIFF_HEAD_DIM), jnp.float32),
        "lambda_k1": 0.1 * jax.random.normal(ks[4], (N_A, DIFF_HEAD_DIM), jnp.float32),
        "lambda_q2": 0.1 * jax.random.normal(ks[5], (N_A, DIFF_HEAD_DIM), jnp.float32),
        "lambda_k2": 0.1 * jax.random.normal(ks[6], (N_A, DIFF_HEAD_DIM), jnp.float32),
        "subln_g": _g(ks[7], (N_A, DIFF_V_DIM)),
        "w_o_a": _w(ks[8], (N_A, DIFF_V_WIDTH, D_MODEL), DIFF_V_WIDTH),
        "kv_in_norm_g": _g(ks[9], (D_MODEL,)),
        "w_dkv": _w(ks[10], (D_MODEL, KV_LORA + MLA_ROPE), D_MODEL),
        "kv_norm_g": _g(ks[11], (KV_LORA,)),
        "w_ukv": _w(ks[12], (KV_LORA, MLA_HEADS * (MLA_NOPE + MLA_V)), KV_LORA),
        "w_dq": _w(ks[13], (N_B, D_MODEL, Q_LORA), D_MODEL),
        "q_norm_g": _g(ks[14], (N_B, Q_LORA)),
        "w_uq": _w(ks[15], (N_B, Q_LORA, MLA_HEADS * (MLA_NOPE + MLA_ROPE)), Q_LORA),
        "w_o_b": _w(ks[16], (N_B, MLA_HEADS * MLA_V, D_MODEL), MLA_HEADS * MLA_V),
        "mlp_norm_g": _g(ks[17], (DEPTH, D_MODEL)),
        "w_up": _w(ks[18], (DEPTH, D_MODEL, D_FF), D_MODEL),
        "w_down": _w(ks[19], (DEPTH, D_FF, D_MODEL), D_FF),
        "final_norm_g": _g(ks[20], (D_MODEL,)),
    }


def reference(x, attn_norm_g, w_qkv_a, lambda_q1, lambda_k1, lambda_q2, lambda_k2, subln_g, w_o_a,
              kv_in_norm_g, w_dkv, kv_norm_g, w_ukv, w_dq, q_norm_g, w_uq, w_o_b,
              mlp_norm_g, w_up, w_down, final_norm_g):
    S = x.shape[1]
    cos_a, sin_a = rope_tables(S, DIFF_HEAD_DIM)
    cos_b, sin_b = rope_tables(S, MLA_ROPE)
    h = x
    k_nope = k_rope = v = None
    for l in range(DEPTH):
        if l < N_A:
            lambda_init = 0.8 - 0.6 * math.exp(-0.3 * l)
            hn = rms_norm(h, attn_norm_g[l])
            h = h + diff_attention(hn, w_qkv_a[l], w_o_a[l], lambda_q1[l], lambda_k1[l],
                                   lambda_q2[l], lambda_k2[l], subln_g[l], lambda_init,
                                   cos_a, sin_a)
        else:
            if l == N_A:
                k_nope, k_rope, v = mla_shared_kv(h, kv_in_norm_g, w_dkv, kv_norm_g, w_ukv,
                                                  cos_b, sin_b)
            j = l - N_A
            hn = rms_norm(h, attn_norm_g[l])
            h = h + mla_attention(hn, w_dq[j], q_norm_g[j], w_uq[j], w_o_b[j],
                                  k_nope, k_rope, v, cos_b, sin_b)
        h = h + sq_relu_mlp(rms_norm(h, mlp_norm_g[l]), w_up[l], w_down[l])
    return rms_norm(h, final_norm_g)
```

```python
import math
from contextlib import ExitStack
import numpy as np
import concourse.bass as bass
import concourse.mybir as mybir
from concourse.bass_utils import run_bass_kernel_spmd

F32 = mybir.dt.float32
BF16 = mybir.dt.bfloat16
AF = mybir.ActivationFunctionType
ALU = mybir.AluOpType

D = 1024
DFF = 4096
EPS = 1e-6
SUBLN_EPS = 1e-5
LAMBDA_INIT = 0.8 - 0.6 * math.exp(-0.3 * 0)
NEG = -30000.0
QS = ["sp", "pe", "act", "dve", "pool"]
CQS = ["pe", "act", "dve", "pool"]


class Buf:
    __slots__ = ("name", "w", "rc", "rd")

    def __init__(self, name=""):
        self.name = name
        self.w = None
        self.rc = {}
        self.rd = []


class Sem:
    __slots__ = ("h", "count", "last")

    def __init__(self, h):
        self.h = h
        self.count = 0
        self.last = None


class Op:
    __slots__ = ("q", "fn", "deps", "needs_sig", "sig", "sem", "is_dma")


class KB:
    def __init__(self, nc, stack):
        self.nc = nc
        self.stack = stack
        self.ops = {q: [] for q in QS}
        self.dsems = []
        self.dfree = []
        self.nsem = 0
        self.esem = {}
        self.new_epoch()

    def _newsem(self):
        self.nsem += 1
        return Sem(self.stack.enter_context(self.nc.semaphore("sm%d" % self.nsem)))

    def new_epoch(self):
        self.esem = {q: self._newsem() for q in CQS}

    def dsem(self):
        if self.dfree:
            return self.dfree.pop()
        s = self._newsem()
        self.dsems.append(s)
        return s

    def op(self, q, fn, reads=(), writes=(), dsem=None):
        o = Op()
        o.q = q
        o.fn = fn
        o.is_dma = dsem is not None
        o.sem = dsem if o.is_dma else self.esem.get(q)
        o.needs_sig = o.is_dma
        o.sig = None
        deps = []

        def add(p, kind):
            if p is None or p is o:
                return
            if (not p.is_dma) and (not o.is_dma) and p.q == q:
                if q == "pe" or kind != "raw":
                    return
            deps.append(p)

        for b in reads:
            add(b.w, "raw")
        for b in writes:
            add(b.w, "waw")
            for r in b.rc.values():
                add(r, "war")
            for r in b.rd:
                add(r, "war")
        if o.is_dma:
            add(dsem.last, "ser")
            dsem.last = o
            dsem.count += 16
            o.sig = dsem.count
        for p in deps:
            p.needs_sig = True
        o.deps = deps
        for b in reads:
            if o.is_dma:
                b.rd.append(o)
            else:
                b.rc[q] = o
        for b in writes:
            b.w = o
            b.rc = {}
            b.rd = []
        self.ops[q].append(o)
        return o

    def barrier(self):
        lasts = []
        for q in CQS:
            for o in reversed(self.ops[q]):
                if (not o.is_dma) and o.fn is not None:
                    lasts.append(o)
                    break
        for s in self.dsems:
            if s.last is not None:
                lasts.append(s.last)
        for q in QS:
            o = Op()
            o.q = q
            o.fn = None
            o.is_dma = False
            o.sem = None
            o.needs_sig = False
            o.sig = None
            o.deps = [p for p in lasts if not (p.q == q and not p.is_dma)]
            for p in o.deps:
                p.needs_sig = True
            self.ops[q].append(o)
        self.dfree = list(self.dsems)
        self.new_epoch()

    def emit(self):
        cnt = {}
        for q in CQS:
            for o in self.ops[q]:
                if o.fn is not None and (not o.is_dma) and o.needs_sig:
                    k = id(o.sem)
                    cnt[k] = cnt.get(k, 0) + 1
                    o.sig = cnt[k]
        ops = self.ops

        def mk(q):
            def body(e):
                known = {}
                for o in ops[q]:
                    for p in o.deps:
                        k = id(p.sem)
                        if known.get(k, 0) < p.sig:
                            e.wait_ge(p.sem.h, p.sig)
                            known[k] = p.sig
                    if o.fn is not None:
                        ins = o.fn(e)
                        if o.needs_sig:
                            ins.then_inc(o.sem.h, 16 if o.is_dma else 1)
            return body

        with self.nc.Block() as block:
            block.sync(mk("sp"))
            block.tensor(mk("pe"))
            block.scalar(mk("act"))
            block.vector(mk("dve"))
            block.gpsimd(mk("pool"))


class TB:
    __slots__ = ("ap", "b")

    def __init__(self, ap, b=None):
        self.ap = ap
        self.b = b if b is not None else Buf()


def build(T, upto=99, debug=False):
    NT = T // 128
    NB = T // 512
    nc = bass.Bass("TRN2", target_bir_lowering=False)

    def din(name, shape):
        return nc.dram_tensor(name, list(shape), F32, kind="ExternalInput").ap()

    x_d = din("x", [T, D])
    wqkv_d = din("w_qkv", [D, 3 * D])
    woa_d = din("w_o_a", [D, D])
    wdkv_d = din("w_dkv", [D, 320])
    wukv_d = din("w_ukv", [256, 2048])
    wdq_d = din("w_dq", [D, 384])
    wuq_d = din("w_uq", [384, 1536])
    wob_d = din("w_o_b", [D, D])
    wup_d = din("w_up", [2, D, DFF])
    wdn_d = din("w_down", [2, DFF, D])
    g_attn_d = [din("g_attn%d" % l, [128, D]) for l in range(2)]
    g_mlp_d = [din("g_mlp%d" % l, [128, D]) for l in range(2)]
    g_kvin_d = din("g_kvin", [128, D])
    g_final_d = din("g_final", [128, D])
    g_kvn_d = din("g_kvn", [128, 256])
    g_q_d = din("g_q", [128, 384])
    g_sub_d = din("g_sub", [128, 1])
    lam_d = din("lam_in", [128, 4, 64])
    ident_d = din("ident", [128, 128])
    tri_d = din("tri", [128, 128])
    cosF_d = din("cosF", [128, T])
    sinF_d = din("sinF", [128, T])
    cosT_d = din("cosT", [128, NT * 32])
    sinT_d = din("sinT", [128, NT * 32])
    kind_s = "ExternalOutput" if debug else "Internal"
    h1_d = nc.dram_tensor("h1", [T, D], F32, kind=kind_s).ap()
    h2_d = nc.dram_tensor("h2", [T, D], F32, kind=kind_s).ap()
    h3_d = nc.dram_tensor("h3", [T, D], F32, kind=kind_s).ap()
    out_d = nc.dram_tensor("out", [T, D], F32, kind="ExternalOutput").ap()

    with ExitStack() as gs:
        kb = KB(nc, gs)

        uniq = [0]

        def sb(st, name, shape, dt):
            uniq[0] += 1
            return st.enter_context(nc.sbuf_tensor("%s_%d" % (name, uniq[0]), list(shape), dt))

        arenaA = sb(gs, "arenaA", [128, 8 * DFF], BF16)
        arenaB = sb(gs, "arenaB", [128, 8 * DFF], BF16)
        ident_bf = TB(sb(gs, "ident_bf", [128, 128], BF16)[:])
        tri_bf = TB(sb(gs, "tri_bf", [128, 128], BF16)[:])
        ones_bf = TB(sb(gs, "ones_bf", [128, 128], BF16)[:])
        ones_f = TB(sb(gs, "ones_f", [128, 128], F32)[:])
        smalls = sb(gs, "smalls", [128, 16], F32)
        nlam = TB(smalls[:, 0:1])
        gsub = TB(smalls[:, 1:2])
        stats_t = sb(gs, "stats", [128, 24], F32)
        stats = [[TB(stats_t[:, 3 * i + j:3 * i + j + 1]) for j in range(3)] for i in range(8)]
        stat_i = [0]
        banks = []
        for i in range(8):
            pt = gs.enter_context(nc.psum_tensor("bank%d" % i, [128, 512], F32))
            banks.append(TB(pt[:]))

        def bf(tb):
            return tb.ap.bitcast(BF16)

        A_ap = arenaA[:]
        B_ap = arenaB[:]

        def dma_in(dst_ap, src_ap, dsem, reads=(), writes=(), q="sp"):
            return kb.op(q, lambda e: e.dma_start(out=dst_ap, in_=src_ap), reads=reads, writes=writes, dsem=dsem)

        def mm(out_ap, lhsT, rhs, start, stop, reads, writes):
            return kb.op("pe", lambda e: e.matmul(out_ap, lhsT, rhs, start=start, stop=stop),
                         reads=reads, writes=writes)

        def act(out_ap, in_ap, func, reads, writes, scale=None, bias=None, accum=None):
            kw = {}
            if scale is not None:
                kw["scale"] = scale
            if bias is not None:
                kw["bias"] = bias
            if accum is not None:
                kw["accum_out"] = accum
            return kb.op("act", lambda e: e.activation(out=out_ap, in_=in_ap, func=func, **kw),
                         reads=reads, writes=writes)

        def tt(q, out_ap, in0, in1, op, reads, writes):
            return kb.op(q, lambda e: e.tensor_tensor(out=out_ap, in0=in0, in1=in1, op=op), reads=reads, writes=writes)

        def stt(out_ap, in0, scalar, in1, op0, op1, reads, writes):
            return kb.op("dve", lambda e: e.scalar_tensor_tensor(out=out_ap, in0=in0, scalar=scalar, in1=in1,
                                                                  op0=op0, op1=op1), reads=reads, writes=writes)

        def cp(q, out_ap, in_ap, reads, writes):
            if q == "act":
                return kb.op("act", lambda e: e.copy(out=out_ap, in_=in_ap), reads=reads, writes=writes)
            return kb.op(q, lambda e: e.tensor_copy(out=out_ap, in_=in_ap), reads=reads, writes=writes)

        def recip(out_ap, in_ap, reads, writes):
            return kb.op("dve", lambda e: e.reciprocal(out=out_ap, in_=in_ap), reads=reads, writes=writes)

        bank_rr = [0]

        def wbank(n=4):
            b = banks[bank_rr[0] % n]
            bank_rr[0] += 1
            return b

        def norm_tile(xt, gB, xn, width, src_ap=None, out_ap=None, eps=EPS):
            src = xt.ap if src_ap is None else src_ap
            dst = xn.ap if out_ap is None else out_ap
            st = stats[stat_i[0] % 8]
            stat_i[0] += 1
            ss, sd, rs = st
            act(dst, src, AF.Square, [xt.b], [xn.b, ss.b], accum=ss.ap)
            act(sd.ap, ss.ap, AF.Sqrt, [ss.b], [sd.b], scale=1.0 / width, bias=eps)
            recip(rs.ap, sd.ap, [sd.b], [rs.b])
            stt(dst, src, rs.ap, gB.ap, ALU.mult, ALU.mult, [xt.b, rs.b, gB.b], [xn.b])

        def transpose_to(xn, n, dst_ap, dst_b, npart=128, rows=128, evq="act"):
            bk = wbank()
            bv = bf(bk)
            for k in range(n):
                o_ap = bv[0:rows, k * 128:(k + 1) * 128]
                i_ap = xn.ap[:, k * rows:(k + 1) * rows]
                kb.op("pe", lambda e, o_ap=o_ap, i_ap=i_ap: e.transpose(o_ap, i_ap, ident_bf.ap),
                      reads=[xn.b, ident_bf.b], writes=[bk.b])
            src = bv[0:rows, 0:n * 128].rearrange("p (k t) -> p k t", k=n)
            cp(evq, dst_ap, src, [bk.b], [dst_b])

        stage_state = {}

        def load_cast(st_list, dst_ap, dst_b, src_ap, shape_free):
            i = stage_state.setdefault(id(st_list), 0)
            stage_state[id(st_list)] = i + 1
            stg, sem = st_list[i % len(st_list)]
            sv = stg.ap
            if shape_free is not None:
                sv = shape_free(stg.ap)
            dma_in(sv, src_ap, sem, writes=[stg.b])
            cp("pool", dst_ap, sv, [stg.b], [dst_b])
            return sv, stg

        with ExitStack() as ps:
            tmpc = TB(sb(ps, "tmpc", [128, 256], F32)[:])
            s0 = kb.dsem()
            dma_in(tmpc.ap[:, 0:128], ident_d, s0, writes=[tmpc.b])
            dma_in(tmpc.ap[:, 128:256], tri_d, s0, writes=[tmpc.b])
            cp("pool", ident_bf.ap, tmpc.ap[:, 0:128], [tmpc.b], [ident_bf.b])
            cp("pool", tri_bf.ap, tmpc.ap[:, 128:256], [tmpc.b], [tri_bf.b])
            kb.op("pool", lambda e: e.memset(ones_bf.ap, 1.0), writes=[ones_bf.b])
            kb.op("pool", lambda e: e.memset(ones_f.ap, 1.0), writes=[ones_f.b])
            lam_t = TB(sb(ps, "lam_t", [128, 4, 64], F32)[:])
            lp = TB(sb(ps, "lp", [128, 2, 64], F32)[:])
            ls = TB(sb(ps, "ls", [128, 4], F32)[:])
            s1 = kb.dsem()
            dma_in(lam_t.ap, lam_d, s1, writes=[lam_t.b])
            dma_in(smalls[:, 1:2], g_sub_d, s1, writes=[gsub.b])
            tt("dve", lp.ap[:, 0, :], lam_t.ap[:, 0, :], lam_t.ap[:, 1, :], ALU.mult, [lam_t.b], [lp.b])
            tt("dve", lp.ap[:, 1, :], lam_t.ap[:, 2, :], lam_t.ap[:, 3, :], ALU.mult, [lam_t.b, lp.b], [lp.b])
            kb.op("dve", lambda e: e.reduce_sum(out=ls.ap[:, 0:2], in_=lp.ap, axis=mybir.AxisListType.X),
                  reads=[lp.b], writes=[ls.b])
            act(ls.ap[:, 2:4], ls.ap[:, 0:2], AF.Exp, [ls.b], [ls.b])
            kb.op("dve", lambda e: e.scalar_tensor_tensor(out=nlam.ap, in0=ls.ap[:, 3:4], scalar=-LAMBDA_INIT,
                                                          in1=ls.ap[:, 2:3], op0=ALU.add, op1=ALU.subtract),
                  reads=[ls.b], writes=[nlam.b])
            kb.op("dve", lambda e: e.tensor_scalar(out=gsub.ap, in0=gsub.ap, scalar1=1.0 - LAMBDA_INIT, scalar2=None,
                                                   op0=ALU.mult), reads=[gsub.b], writes=[gsub.b])
            kb.barrier()

        def pre_pass_simple(src_d, g_d, dstT):
            with ExitStack() as ps:
                xts = [(TB(sb(ps, "xt%d" % i, [128, D], F32)[:]), kb.dsem()) for i in range(3)]
                xns = [TB(sb(ps, "xn%d" % i, [128, D], BF16)[:]) for i in range(2)]
                gB = TB(sb(ps, "gB", [128, D], F32)[:])
                sg = kb.dsem()
                dma_in(gB.ap, g_d, sg, writes=[gB.b])
                tb = [Buf() for _ in range(NT)]

                def ld(i):
                    xt, sem = xts[i % 3]
                    dma_in(xt.ap, src_d[i * 128:(i + 1) * 128, :], sem, writes=[xt.b])
                ld(0)
                if NT > 1:
                    ld(1)
                for i in range(NT):
                    if i + 2 < NT:
                        ld(i + 2)
                    xt = xts[i % 3][0]
                    xn = xns[i % 2]
                    norm_tile(xt, gB, xn, D)
                    transpose_to(xn, 8, dstT[:, :, i * 128:(i + 1) * 128], tb[i], evq=("act" if i % 2 else "dve"))
                kb.barrier()
            return tb

        def wo_pass(src_d, w_d, OT, dst_d, prefetch=None):
            with ExitStack() as ps:
                xts = [(TB(sb(ps, "xt%d" % i, [128, D], F32)[:]), kb.dsem(), kb.dsem()) for i in range(3)]
                stg = [(TB(sb(ps, "wstg%d" % i, [128, D], F32)[:]), kb.dsem()) for i in range(2)]
                wo = TB(sb(ps, "wo_b", [128, 8, D], BF16)[:])
                for h in range(8):
                    load_cast(stg, wo.ap[:, h, :], wo.b, w_d[h * 128:(h + 1) * 128, :], None)

                def ld(i):
                    xt, sem, _ = xts[i % 3]
                    dma_in(xt.ap, src_d[i * 128:(i + 1) * 128, :], sem, writes=[xt.b])
                ld(0)
                if NT > 1:
                    ld(1)
                for i in range(NT):
                    if i + 2 < NT:
                        ld(i + 2)
                    xt, _, ssem = xts[i % 3]
                    for half in range(2):
                        bk = wbank(8)
                        for h in range(8):
                            mm(bk.ap, OT[:, h, i * 128:(i + 1) * 128], wo.ap[:, h, half * 512:(half + 1) * 512],
                               h == 0, h == 7, [wo.b], [bk.b])
                        tt("dve", xt.ap[:, half * 512:(half + 1) * 512], bk.ap, xt.ap[:, half * 512:(half + 1) * 512],
                           ALU.add, [bk.b, xt.b], [xt.b])
                    dma_in(dst_d[i * 128:(i + 1) * 128, :], xt.ap, ssem, reads=[xt.b], q="pool")
                    if prefetch is not None:
                        prefetch(i, stg)
                kb.barrier()

        def mlp_pass(src_d, l, g_d, dst_d, wup_loaded, final_g=None):
            with ExitStack() as ps:
                Wup = A_ap[:, 0:8 * DFF].rearrange("p (k f) -> p k f", k=8)
                Wdn = B_ap[:, 0:32 * D].rearrange("p (k f) -> p k f", k=32)
                wupB = wup_loaded if wup_loaded is not None else Buf()
                wdnB = [Buf() for _ in range(32)]
                hts = [(TB(sb(ps, "ht%d" % i, [128, D], F32)[:]), kb.dsem(), kb.dsem()) for i in range(4)]
                xns = [TB(sb(ps, "xn%d" % i, [128, D], BF16)[:]) for i in range(2)]
                hnT = sb(ps, "hnTb", [128, 8, 512], BF16)[:]
                hnB = [Buf() for _ in range(4)]
                aT = sb(ps, "aT", [128, 32, 512], BF16)[:]
                aB = [Buf() for _ in range(32)]
                rts = [TB(sb(ps, "rt%d" % i, [128, 512], BF16)[:]) for i in range(2)]
                gB = TB(sb(ps, "gB", [128, D], F32)[:])
                stg = [(TB(sb(ps, "wstg%d" % i, [128, 512], F32)[:]), kb.dsem()) for i in range(2)]
                sg = kb.dsem()
                dma_in(gB.ap, g_d, sg, writes=[gB.b])
                gF = None
                if final_g is not None:
                    gF = TB(sb(ps, "gF", [128, D], F32)[:])
                    dma_in(gF.ap, final_g, sg, writes=[gF.b])

                def ld_block(b):
                    for j in range(4):
                        ht, sem, _ = hts[j]
                        i = 4 * b + j
                        dma_in(ht.ap, src_d[i * 128:(i + 1) * 128, :], sem, writes=[ht.b])
                ld_block(0)
                if wup_loaded is None:
                    for dk in range(8):
                        for qq in range(8):
                            load_cast(stg, Wup[:, dk, qq * 512:(qq + 1) * 512], wupB,
                                      wup_d[l, dk * 128:(dk + 1) * 128, qq * 512:(qq + 1) * 512], None)
                for f in range(32):
                    for hf in range(2):
                        load_cast(stg, Wdn[:, f, hf * 512:(hf + 1) * 512], wdnB[f],
                                  wdn_d[l, f * 128:(f + 1) * 128, hf * 512:(hf + 1) * 512], None)
                for b in range(NB):
                    if b > 0:
                        ld_block(b)
                    for j in range(4):
                        ht = hts[j][0]
                        xn = xns[j % 2]
                        norm_tile(ht, gB, xn, D)
                        transpose_to(xn, 8, hnT[:, :, j * 128:(j + 1) * 128], hnB[j], evq=("act" if j % 2 else "dve"))
                    for f in range(32):
                        bk = wbank(4)
                        for dk in range(8):
                            mm(bk.ap, Wup[:, dk, f * 128:(f + 1) * 128], hnT[:, dk, :], dk == 0, dk == 7,
                               [wupB] + hnB, [bk.b])
                        rt = rts[f % 2]
                        act(rt.ap, bk.ap, AF.Relu, [bk.b], [rt.b])
                        tt("dve", aT[:, f, :], rt.ap, rt.ap, ALU.mult, [rt.b], [aB[f]])
                    for j in range(4):
                        ht, _, ssem = hts[j]
                        for half in range(2):
                            bk = banks[4 + (2 * j + half) % 4]
                            for f in range(32):
                                mm(bk.ap, aT[:, f, j * 128:(j + 1) * 128], Wdn[:, f, half * 512:(half + 1) * 512],
                                   f == 0, f == 31, [aB[f], wdnB[f]], [bk.b])
                            tt("dve", ht.ap[:, half * 512:(half + 1) * 512], bk.ap,
                               ht.ap[:, half * 512:(half + 1) * 512], ALU.add, [bk.b, ht.b], [ht.b])
                        i = 4 * b + j
                        if gF is None:
                            dma_in(dst_d[i * 128:(i + 1) * 128, :], ht.ap, ssem, reads=[ht.b], q="pool")
                        else:
                            st = stats[stat_i[0] % 8]
                            stat_i[0] += 1
                            ss, sd, rs = st
                            xn = xns[j % 2]
                            act(xn.ap, ht.ap, AF.Square, [ht.b], [xn.b, ss.b], accum=ss.ap)
                            act(sd.ap, ss.ap, AF.Sqrt, [ss.b], [sd.b], scale=1.0 / D, bias=EPS)
                            recip(rs.ap, sd.ap, [sd.b], [rs.b])
                            stt(ht.ap, ht.ap, rs.ap, gF.ap, ALU.mult, ALU.mult, [ht.b, rs.b, gF.b], [ht.b])
                            dma_in(dst_d[i * 128:(i + 1) * 128, :], ht.ap, ssem, reads=[ht.b], q="pool")
                kb.barrier()

        def attention_block(qb, comps, Vh, vB, E, scale, pending):
            nch = 4 * qb + 4
            q0 = qb * 512

            def cols_of(kc):
                j = kc - 4 * qb
                return (128 * j if j > 0 else 0), j

            def qk(kc):
                lo, j = cols_of(kc)
                for c, cm in enumerate(comps):
                    S = cm["S"][kc % 2]
                    parts = cm["parts"]
                    for pi, (KT, QT, bufs) in enumerate(parts):
                        mm(S.ap[:, lo:512], KT[:, kc * 128:(kc + 1) * 128], QT[:, q0 + lo:q0 + 512],
                           pi == 0, (pi == len(parts) - 1) and j < 0, bufs, [S.b])
                    if j >= 0:
                        mm(S.ap[:, lo:lo + 128], ident_bf.ap, tri_bf.ap, False, True,
                           [ident_bf.b, tri_bf.b], [S.b])
                    e = E[c][kc % 3]
                    act(e.ap[:, lo:512], S.ap[:, lo:512], AF.Exp, [S.b], [e.b], scale=scale)

            def pv(kc):
                lo, j = cols_of(kc)
                for c, cm in enumerate(comps):
                    e = E[c][kc % 3]
                    mm(cm["O"].ap[:, lo:512], Vh[:, kc, :], e.ap[:, lo:512], kc == 0, kc == nch - 1,
                       [e.b, vB], [cm["O"].b])
                    mm(cm["L"].ap[:, lo:512], ones_bf.ap, e.ap[:, lo:512], kc == 0, kc == nch - 1,
                       [e.b, ones_bf.b], [cm["L"].b])

            qk(0)
            for kc in range(nch):
                if kc + 1 < nch:
                    qk(kc + 1)
                pv(kc)
                if kc == nch - 2 or nch == 1:
                    while pending:
                        pending.pop(0)()

        hnT0 = A_ap[:, 0:8 * T].rearrange("p (k t) -> p k t", k=8)
        OnT = B_ap[:, 0:8 * T].rearrange("p (k t) -> p k t", k=8)
        if upto >= 1:
            hnB0 = pre_pass_simple(x_d, g_attn_d[0], hnT0)
        if upto >= 2:
            with ExitStack() as ps:
                stg = [(TB(sb(ps, "wstg%d" % i, [128, 8, 128], F32)[:]), kb.dsem()) for i in range(2)]
                wb = TB(sb(ps, "wb", [128, 5, 8, 128], BF16)[:])
                QT = TB(sb(ps, "QT", [128, T], BF16)[:])
                KT = TB(sb(ps, "KT", [128, T], BF16)[:])
                Vh = TB(sb(ps, "Vh", [128, NT, 128], BF16)[:])
                css = [(TB(sb(ps, "cs%d" % i, [128, 2, 512], F32)[:]), kb.dsem()) for i in range(2)]
                tA = TB(sb(ps, "tA", [128, 512], F32)[:])
                tB_ = TB(sb(ps, "tB", [128, 512], F32)[:])
                E = [[TB(sb(ps, "E%d_%d" % (c, i), [128, 512], BF16)[:]) for i in range(3)] for c in range(2)]
                cL = [TB(sb(ps, "cL%d" % c, [128, 512], F32)[:]) for c in range(2)]
                cO = [TB(sb(ps, "cO%d" % c, [128, 512], F32)[:]) for c in range(2)]
                sq = TB(sb(ps, "sq", [128, 512], F32)[:])
                onB = [[Buf() for _ in range(NB)] for _ in range(8)]
                pending = []

                def load_w(h):
                    for s in range(3):
                        src = wqkv_d[:, s * 1024 + h * 128:s * 1024 + (h + 1) * 128].rearrange(
                            "(k p) j -> p k j", p=128)
                        sv, st = load_cast(stg, wb.ap[:, s, :, :], wb.b, src, None)
                        if s < 2:
                            s6 = sv.rearrange("p k (c h d) -> p k c h d", c=2, h=2)
                            d6 = wb.ap[:, 3 + s, :, :].rearrange("p k (c h d) -> p k c h d", c=2, h=2)
                            for c in range(2):
                                kb.op("pool", lambda e, o=d6[:, :, c, 0, :], i=s6[:, :, c, 1, :]:
                                      e.tensor_scalar(out=o, in0=i, scalar1=-1.0, scalar2=0.0, op0=ALU.mult,
                                                      op1=ALU.add), reads=[st.b], writes=[wb.b])
                                cp("pool", d6[:, :, c, 1, :], s6[:, :, c, 0, :], [st.b], [wb.b])

                def ld_cs(b):
                    cs, sem = css[b % 2]
                    dma_in(cs.ap[:, 0, :], cosF_d[:, b * 512:(b + 1) * 512], sem, writes=[cs.b])
                    dma_in(cs.ap[:, 1, :], sinF_d[:, b * 512:(b + 1) * 512], sem, writes=[cs.b])

                def proj(h):
                    ld_cs(0)
                    for b in range(NB):
                        if b + 1 < NB:
                            ld_cs(b + 1)
                        cs = css[b % 2][0]
                        hb = hnB0[4 * b:4 * b + 4]
                        for (wi, dst) in ((0, QT), (1, KT)):
                            bA = wbank(4)
                            bB = wbank(4)
                            for dk in range(8):
                                mm(bA.ap, wb.ap[:, wi, dk, :], hnT0[:, dk, b * 512:(b + 1) * 512], dk == 0, dk == 7,
                                   [wb.b] + hb, [bA.b])
                            for dk in range(8):
                                mm(bB.ap, wb.ap[:, 3 + wi, dk, :], hnT0[:, dk, b * 512:(b + 1) * 512], dk == 0,
                                   dk == 7, [wb.b] + hb, [bB.b])
                            tt("dve", tA.ap, bA.ap, cs.ap[:, 0, :], ALU.mult, [bA.b, cs.b], [tA.b])
                            tt("dve", tB_.ap, bB.ap, cs.ap[:, 1, :], ALU.mult, [bB.b, cs.b], [tB_.b])
                            tt("dve", dst.ap[:, b * 512:(b + 1) * 512], tA.ap, tB_.ap, ALU.add, [tA.b, tB_.b], [dst.b])
                        bV = wbank(4)
                        for j in range(4):
                            for dk in range(8):
                                mm(bV.ap[:, j * 128:(j + 1) * 128], hnT0[:, dk, (4 * b + j) * 128:(4 * b + j + 1) * 128],
                                   wb.ap[:, 2, dk, :], dk == 0, dk == 7, [wb.b] + hb, [bV.b])
                        cp("act", Vh.ap[:, 4 * b:4 * b + 4, :], bV.ap.rearrange("p (j e) -> p j e", j=4), [bV.b], [Vh.b])

                def epilogue(h, qb, comps):
                    cp("act", cL[0].ap, comps[0]["L"].ap, [comps[0]["L"].b], [cL[0].b])
                    cp("act", cL[1].ap, comps[1]["L"].ap, [comps[1]["L"].b], [cL[1].b])
                    cp("dve", cO[0].ap, comps[0]["O"].ap, [comps[0]["O"].b], [cO[0].b])
                    cp("dve", cO[1].ap, comps[1]["O"].ap, [comps[1]["O"].b], [cO[1].b])
                    for c in range(2):
                        recip(cL[c].ap, cL[c].ap, [cL[c].b], [cL[c].b])
                    for c in range(2):
                        tt("dve", cO[c].ap, cO[c].ap, cL[c].ap, ALU.mult, [cO[c].b, cL[c].b], [cO[c].b])
                    stt(cO[0].ap, cO[1].ap, nlam.ap, cO[0].ap, ALU.mult, ALU.add, [cO[0].b, cO[1].b, nlam.b], [cO[0].b])
                    act(sq.ap, cO[0].ap, AF.Square, [cO[0].b], [sq.b])

                    def tail():
                        bk = banks[0]
                        mm(bk.ap, ones_f.ap, sq.ap, True, True, [ones_f.b, sq.b], [bk.b])
                        act(cL[0].ap, bk.ap, AF.Ln, [bk.b], [cL[0].b], scale=1.0 / 128, bias=SUBLN_EPS)
                        act(cL[0].ap, cL[0].ap, AF.Exp, [cL[0].b], [cL[0].b], scale=-0.5)
                        stt(OnT[:, h, qb * 512:(qb + 1) * 512], cO[0].ap, gsub.ap, cL[0].ap, ALU.mult, ALU.mult,
                            [cO[0].b, gsub.b, cL[0].b], [onB[h][qb]])
                    pending.append(tail)

                load_w(0)
                for h in range(8):
                    proj(h)
                    while pending:
                        pending.pop(0)()
                    if h + 1 < 8:
                        load_w(h + 1)
                    comps = []
                    for c in range(2):
                        comps.append({
                            "parts": [(KT.ap[c * 64:(c + 1) * 64, :], QT.ap[c * 64:(c + 1) * 64, :], [KT.b, QT.b])],
                            "S": [banks[2 * c], banks[2 * c + 1]], "O": banks[4 + c], "L": banks[6 + c]})
                    for qb in range(NB):
                        attention_block(qb, comps, Vh.ap, Vh.b, E, 0.125, pending)
                        epilogue(h, qb, comps)
                while pending:
                    pending.pop(0)()
                kb.barrier()
        wupB0 = None
        if upto >= 3:
            Wup = A_ap[:, 0:8 * DFF].rearrange("p (k f) -> p k f", k=8)
            wupB0 = Buf()

            def pf0(i, stg):
                per = (32 + NT - 1) // NT
                for pi in range(i * per, min(32, (i + 1) * per)):
                    dk, qq = pi // 4, pi % 4
                    load_cast(stg, Wup[:, dk, qq * 1024:(qq + 1) * 1024], wupB0,
                              wup_d[0, dk * 128:(dk + 1) * 128, qq * 1024:(qq + 1) * 1024], None)
            wo_pass(x_d, woa_d, OnT, h1_d, prefetch=pf0)
        if upto >= 4:
            mlp_pass(h1_d, 0, g_mlp_d[0], h2_d, wupB0)

        cqnT = A_ap[:, 0:3 * T].rearrange("p (k t) -> p k t", k=3)
        cnT = A_ap[:, 3 * T:5 * T].rearrange("p (k t) -> p k t", k=2)
        krT = A_ap[:, 5 * T:6 * T]
        OT1 = B_ap[:, 0:8 * T].rearrange("p (k t) -> p k t", k=8)
        if upto >= 5:
            with ExitStack() as ps:
                xts = [(TB(sb(ps, "xt%d" % i, [128, D], F32)[:]), kb.dsem()) for i in range(3)]
                xns = [TB(sb(ps, "xn%d" % i, [128, D], BF16)[:]) for i in range(2)]
                gKV = TB(sb(ps, "gKV", [128, D], F32)[:])
                gA1 = TB(sb(ps, "gA1", [128, D], F32)[:])
                gkn = TB(sb(ps, "gkn", [128, 256], F32)[:])
                gq = TB(sb(ps, "gq", [128, 384], F32)[:])
                csT = TB(sb(ps, "csT", [128, 2, NT, 32], F32)[:])
                stg = [(TB(sb(ps, "wstg%d" % i, [128, 8, 384], F32)[:]), kb.dsem()) for i in range(2)]
                wdkv = TB(sb(ps, "wdkv", [128, 8, 320], BF16)[:])
                wdq = TB(sb(ps, "wdq", [128, 8, 384], BF16)[:])
                hTs = [TB(sb(ps, "hT%d" % i, [128, 8, 128], BF16)[:]) for i in range(2)]
                cnb = TB(sb(ps, "cnb", [128, 384], BF16)[:])
                krb = TB(sb(ps, "krb", [128, 64], BF16)[:])
                rtmp = TB(sb(ps, "rtmp", [128, 4, 32], F32)[:])
                sg = kb.dsem()
                dma_in(gKV.ap, g_kvin_d, sg, writes=[gKV.b])
                dma_in(gA1.ap, g_attn_d[1], sg, writes=[gA1.b])
                dma_in(gkn.ap, g_kvn_d, sg, writes=[gkn.b])
                dma_in(gq.ap, g_q_d, sg, writes=[gq.b])
                dma_in(csT.ap[:, 0, :, :], cosT_d.rearrange("p (n f) -> p n f", f=32), sg, writes=[csT.b])
                dma_in(csT.ap[:, 1, :, :], sinT_d.rearrange("p (n f) -> p n f", f=32), sg, writes=[csT.b])
                load_cast(stg, wdkv.ap, wdkv.b, wdkv_d.rearrange("(k p) j -> p k j", p=128), lambda a: a[:, :, 0:320])
                load_cast(stg, wdq.ap, wdq.b, wdq_d.rearrange("(k p) j -> p k j", p=128), None)
                preB = [Buf() for _ in range(NT)]

                def ld(i):
                    xt, sem = xts[i % 3]
                    dma_in(xt.ap, h2_d[i * 128:(i + 1) * 128, :], sem, writes=[xt.b])
                ld(0)
                if NT > 1:
                    ld(1)
                for i in range(NT):
                    if i + 2 < NT:
                        ld(i + 2)
                    xt = xts[i % 3][0]
                    xn = xns[0]
                    norm_tile(xt, gKV, xn, D)
                    hT = hTs[0]
                    transpose_to(xn, 8, hT.ap, hT.b, evq="act")
                    bk = wbank(8)
                    for dk in range(8):
                        mm(bk.ap[:, 0:320], hT.ap[:, dk, :], wdkv.ap[:, dk, :], dk == 0, dk == 7, [hT.b, wdkv.b], [bk.b])
                    norm_tile(bk, gkn, cnb, 256, src_ap=bk.ap[:, 0:256], out_ap=cnb.ap[:, 0:256])
                    x1 = bk.ap[:, 256:288]
                    x2 = bk.ap[:, 288:320]
                    cc = csT.ap[:, 0, i, :]
                    sn = csT.ap[:, 1, i, :]
                    tt("dve", rtmp.ap[:, 0, :], x1, cc, ALU.mult, [bk.b, csT.b], [rtmp.b])
                    tt("dve", rtmp.ap[:, 1, :], x2, sn, ALU.mult, [bk.b, csT.b, rtmp.b], [rtmp.b])
                    tt("dve", rtmp.ap[:, 2, :], x2, cc, ALU.mult, [bk.b, csT.b, rtmp.b], [rtmp.b])
                    tt("dve", rtmp.ap[:, 3, :], x1, sn, ALU.mult, [bk.b, csT.b, rtmp.b], [rtmp.b])
                    tt("dve", krb.ap[:, 0:32], rtmp.ap[:, 0, :], rtmp.ap[:, 1, :], ALU.subtract, [rtmp.b], [krb.b])
                    tt("dve", krb.ap[:, 32:64], rtmp.ap[:, 2, :], rtmp.ap[:, 3, :], ALU.add, [rtmp.b, krb.b], [krb.b])
                    b2 = wbank(8)
                    bv = bf(b2)
                    for k in range(2):
                        kb.op("pe", lambda e, o=bv[:, k * 128:(k + 1) * 128], a=cnb.ap[:, k * 128:(k + 1) * 128]:
                              e.transpose(o, a, ident_bf.ap), reads=[cnb.b, ident_bf.b], writes=[b2.b])
                    kb.op("pe", lambda e, o=bv[0:64, 256:384], a=krb.ap: e.transpose(o, a, ident_bf.ap),
                          reads=[krb.b, ident_bf.b], writes=[b2.b])
                    cp("dve", cnT[:, :, i * 128:(i + 1) * 128], bv[:, 0:256].rearrange("p (k t) -> p k t", k=2),
                       [b2.b], [preB[i]])
                    cp("dve", krT[0:64, i * 128:(i + 1) * 128], bv[0:64, 256:384], [b2.b], [preB[i]])
                    xn = xns[1]
                    norm_tile(xt, gA1, xn, D)
                    hT = hTs[1]
                    transpose_to(xn, 8, hT.ap, hT.b, evq="act")
                    bk = wbank(8)
                    for dk in range(8):
                        mm(bk.ap[:, 0:384], hT.ap[:, dk, :], wdq.ap[:, dk, :], dk == 0, dk == 7, [hT.b, wdq.b], [bk.b])
                    norm_tile(bk, gq, cnb, 384, src_ap=bk.ap[:, 0:384])
                    b2 = wbank(8)
                    bv = bf(b2)
                    for k in range(3):
                        kb.op("pe", lambda e, o=bv[:, k * 128:(k + 1) * 128], a=cnb.ap[:, k * 128:(k + 1) * 128]:
                              e.transpose(o, a, ident_bf.ap), reads=[cnb.b, ident_bf.b], writes=[b2.b])
                    cp("dve", cqnT[:, :, i * 128:(i + 1) * 128], bv[:, 0:384].rearrange("p (k t) -> p k t", k=3),
                       [b2.b], [preB[i]])
                kb.barrier()
        if upto >= 6:
            with ExitStack() as ps:
                stq = [(TB(sb(ps, "stq", [128, 3, 192], F32)[:]), kb.dsem())]
                stkv = [(TB(sb(ps, "stkv", [128, 2, 256], F32)[:]), kb.dsem())]
                wq = TB(sb(ps, "wq", [128, 3, 256], BF16)[:])
                wkv = TB(sb(ps, "wkv", [128, 2, 256], BF16)[:])
                qnT = TB(sb(ps, "qnT", [128, T], BF16)[:])
                knT = TB(sb(ps, "knT", [128, T], BF16)[:])
                qrT = TB(sb(ps, "qrT", [128, T], BF16)[:])
                Vh = TB(sb(ps, "Vh", [128, NT, 128], BF16)[:])
                css = [(TB(sb(ps, "cs%d" % i, [128, 2, 512], F32)[:]), kb.dsem()) for i in range(2)]
                tA = TB(sb(ps, "tA", [128, 512], F32)[:])
                tB_ = TB(sb(ps, "tB", [128, 512], F32)[:])
                E = [[TB(sb(ps, "E%d" % i, [128, 512], BF16)[:]) for i in range(3)]]
                rL = TB(sb(ps, "rL", [128, 512], F32)[:])
                oB = [[Buf() for _ in range(NB)] for _ in range(8)]
                pending = []
                allpre = preB
                krB = Buf()

                def load_w1(h):
                    sv, st = load_cast(stq, wq.ap[:, :, 0:192], wq.b,
                                       wuq_d[:, h * 192:(h + 1) * 192].rearrange("(k p) j -> p k j", p=128), None)
                    kb.op("pool", lambda e, o=wq.ap[:, :, 192:224], i=sv[:, :, 160:192]:
                          e.tensor_scalar(out=o, in0=i, scalar1=-1.0, scalar2=0.0, op0=ALU.mult, op1=ALU.add),
                          reads=[st.b], writes=[wq.b])
                    cp("pool", wq.ap[:, :, 224:256], sv[:, :, 128:160], [st.b], [wq.b])
                    load_cast(stkv, wkv.ap, wkv.b,
                              wukv_d[:, h * 256:(h + 1) * 256].rearrange("(k p) j -> p k j", p=128), None)

                def ld_cs(b):
                    cs, sem = css[b % 2]
                    dma_in(cs.ap[0:64, 0, :], cosF_d[0:64, b * 512:(b + 1) * 512], sem, writes=[cs.b])
                    dma_in(cs.ap[0:64, 1, :], sinF_d[0:64, b * 512:(b + 1) * 512], sem, writes=[cs.b])

                def proj1(h):
                    ld_cs(0)
                    for b in range(NB):
                        if b + 1 < NB:
                            ld_cs(b + 1)
                        cs = css[b % 2][0]
                        pb = allpre[4 * b:4 * b + 4]
                        blk = slice(b * 512, (b + 1) * 512)
                        bk = wbank(4)
                        for k in range(3):
                            mm(bk.ap, wq.ap[:, k, 0:128], cqnT[:, k, blk], k == 0, k == 2, [wq.b] + pb, [bk.b])
                        cp("act", qnT.ap[:, blk], bk.ap, [bk.b], [qnT.b])
                        bA = wbank(4)
                        bB = wbank(4)
                        for k in range(3):
                            mm(bA.ap[0:64, :], wq.ap[:, k, 128:192], cqnT[:, k, blk], k == 0, k == 2, [wq.b] + pb, [bA.b])
                        for k in range(3):
                            mm(bB.ap[0:64, :], wq.ap[:, k, 192:256], cqnT[:, k, blk], k == 0, k == 2, [wq.b] + pb, [bB.b])
                        tt("dve", tA.ap[0:64, :], bA.ap[0:64, :], cs.ap[0:64, 0, :], ALU.mult, [bA.b, cs.b], [tA.b])
                        tt("dve", tB_.ap[0:64, :], bB.ap[0:64, :], cs.ap[0:64, 1, :], ALU.mult, [bB.b, cs.b], [tB_.b])
                        tt("dve", qrT.ap[0:64, blk], tA.ap[0:64, :], tB_.ap[0:64, :], ALU.add, [tA.b, tB_.b], [qrT.b])
                        bk = wbank(4)
                        for k in range(2):
                            mm(bk.ap, wkv.ap[:, k, 0:128], cnT[:, k, blk], k == 0, k == 1, [wkv.b] + pb, [bk.b])
                        cp("act", knT.ap[:, blk], bk.ap, [bk.b], [knT.b])
                        bV = wbank(4)
                        for j in range(4):
                            tl = slice((4 * b + j) * 128, (4 * b + j + 1) * 128)
                            for k in range(2):
                                mm(bV.ap[:, j * 128:(j + 1) * 128], cnT[:, k, tl], wkv.ap[:, k, 128:256], k == 0, k == 1,
                                   [wkv.b] + pb, [bV.b])
                        cp("act", Vh.ap[:, 4 * b:4 * b + 4, :], bV.ap.rearrange("p (j e) -> p j e", j=4), [bV.b], [Vh.b])

                load_w1(0)
                sc1 = 192.0 ** -0.5
                for h in range(8):
                    proj1(h)
                    if h + 1 < 8:
                        load_w1(h + 1)
                    for qb in range(NB):
                        acc = 4 + 2 * (qb % 2)
                        comps = [{
                            "parts": [(knT.ap, qnT.ap, [knT.b, qnT.b]),
                                      (krT[0:64, :], qrT.ap[0:64, :], [qrT.b] + allpre)],
                            "S": [banks[0], banks[1]], "O": banks[acc], "L": banks[acc + 1]}]
                        attention_block(qb, comps, Vh.ap, Vh.b, E, sc1, pending)
                        cm = comps[0]
                        recip(rL.ap, cm["L"].ap, [cm["L"].b], [rL.b])
                        tt("dve", OT1[:, h, qb * 512:(qb + 1) * 512], cm["O"].ap, rL.ap, ALU.mult,
                           [cm["O"].b, rL.b], [oB[h][qb]])
                kb.barrier()
        wupB1 = None
        if upto >= 7:
            Wup = A_ap[:, 0:8 * DFF].rearrange("p (k f) -> p k f", k=8)
            wupB1 = Buf()

            def pf1(i, stg):
                per = (32 + NT - 1) // NT
                for pi in range(i * per, min(32, (i + 1) * per)):
                    dk, qq = pi // 4, pi % 4
                    load_cast(stg, Wup[:, dk, qq * 1024:(qq + 1) * 1024], wupB1,
                              wup_d[1, dk * 128:(dk + 1) * 128, qq * 1024:(qq + 1) * 1024], None)
            wo_pass(h2_d, wob_d, OT1, h3_d, prefetch=pf1)
        if upto >= 8:
            mlp_pass(h3_d, 1, g_mlp_d[1], out_d, wupB1, final_g=g_final_d)
        kb.barrier()
        kb.emit()
    return nc


def _consts(T):
    ident = np.eye(128, dtype=np.float32)
    p = np.arange(128)[:, None]
    f = np.arange(128)[None, :]
    tri = np.where(f >= p, 0.0, NEG).astype(np.float32)
    pos = np.arange(T, dtype=np.float32)
    inv = (np.float32(10000.0) ** (-(np.arange(0, 64, 2, dtype=np.float32)) / np.float32(64))).astype(np.float32)
    ang = (pos[:, None] * inv[None, :]).astype(np.float32)
    cosT = np.cos(ang).astype(np.float32)
    sinT = np.sin(ang).astype(np.float32)
    idx = np.arange(128) % 32
    cosF = np.ascontiguousarray(cosT[:, idx].T)
    sinF = np.ascontiguousarray(sinT[:, idx].T)
    NT = T // 128
    cosL = np.ascontiguousarray(cosT.reshape(NT, 128, 32).transpose(1, 0, 2).reshape(128, NT * 32))
    sinL = np.ascontiguousarray(sinT.reshape(NT, 128, 32).transpose(1, 0, 2).reshape(128, NT * 32))
    return dict(ident=ident, tri=tri, cosF=cosF, sinF=sinF, cosT=cosL, sinT=sinL)


def _rep(v, n=128):
    v = np.asarray(v, dtype=np.float32).reshape(1, -1)
    return np.ascontiguousarray(np.broadcast_to(v, (n, v.shape[1])))


def make_in_maps(inp, T, ncores):
    c = _consts(T)
    f = lambda a: np.ascontiguousarray(np.asarray(a, dtype=np.float32))
    shared = dict(
        w_qkv=f(inp["w_qkv_a"][0]), w_o_a=f(inp["w_o_a"][0]), w_dkv=f(inp["w_dkv"]), w_ukv=f(inp["w_ukv"]),
        w_dq=f(inp["w_dq"][0]), w_uq=f(inp["w_uq"][0]), w_o_b=f(inp["w_o_b"][0]),
        w_up=f(inp["w_up"]), w_down=f(inp["w_down"]),
        g_attn0=_rep(inp["attn_norm_g"][0]), g_attn1=_rep(inp["attn_norm_g"][1]),
        g_mlp0=_rep(inp["mlp_norm_g"][0]), g_mlp1=_rep(inp["mlp_norm_g"][1]),
        g_kvin=_rep(inp["kv_in_norm_g"]), g_final=_rep(inp["final_norm_g"]),
        g_kvn=_rep(inp["kv_norm_g"]), g_q=_rep(inp["q_norm_g"][0]),
        g_sub=np.ascontiguousarray(np.asarray(inp["subln_g"][0], dtype=np.float32).reshape(128, 1)),
        lam_in=np.ascontiguousarray(np.stack([_rep(inp["lambda_q1"][0]), _rep(inp["lambda_k1"][0]),
                                              _rep(inp["lambda_q2"][0]), _rep(inp["lambda_k2"][0])], axis=1)),
        **c)
    x = np.asarray(inp["x"], dtype=np.float32)
    maps = []
    for b in range(ncores):
        m = dict(shared)
        m["x"] = np.ascontiguousarray(x[b, :T])
        maps.append(m)
    return maps


def kernel(**inputs):
    T = 4096
    nc = build(T)
    maps = make_in_maps(inputs, T, 8)
    res = run_bass_kernel_spmd(nc, maps, core_ids=list(range(8)))
    out = np.stack([np.asarray(r["out"], dtype=np.float32).reshape(T, D) for r in res.results], axis=0)
    return out
```

```python
import math
from contextlib import ExitStack
import numpy as np
import concourse.bass as bass
import concourse.mybir as mybir
from concourse.bass_utils import run_bass_kernel_spmd

F32 = mybir.dt.float32
BF16 = mybir.dt.bfloat16
AF = mybir.ActivationFunctionType
ALU = mybir.AluOpType

D = 1024
DFF = 4096
EPS = 1e-6
SUBLN_EPS = 1e-5
LAMBDA_INIT = 0.8 - 0.6 * math.exp(-0.3 * 0)
NEG = -30000.0
QS = ["sp", "pe", "act", "dve", "pool"]
CQS = ["pe", "act", "dve", "pool"]


class Buf:
    __slots__ = ("name", "w", "rc", "rd")

    def __init__(self, name=""):
        self.name = name
        self.w = None
        self.rc = {}
        self.rd = []


class Sem:
    __slots__ = ("h", "count", "last")

    def __init__(self, h):
        self.h = h
        self.count = 0
        self.last = None


class Op:
    __slots__ = ("q", "fn", "deps", "needs_sig", "sig", "sem", "is_dma")


class KB:
    def __init__(self, nc, stack):
        self.nc = nc
        self.stack = stack
        self.ops = {q: [] for q in QS}
        self.dsems = []
        self.dfree = []
        self.nsem = 0
        self.esem = {}
        self.new_epoch()

    def _newsem(self):
        self.nsem += 1
        return Sem(self.stack.enter_context(self.nc.semaphore("sm%d" % self.nsem)))

    def new_epoch(self):
        self.esem = {q: self._newsem() for q in CQS}

    def dsem(self):
        if self.dfree:
            return self.dfree.pop()
        s = self._newsem()
        self.dsems.append(s)
        return s

    def op(self, q, fn, reads=(), writes=(), dsem=None):
        o = Op()
        o.q = q
        o.fn = fn
        o.is_dma = dsem is not None
        o.sem = dsem if o.is_dma else self.esem.get(q)
        o.needs_sig = o.is_dma
        o.sig = None
        deps = []

        def add(p, kind):
            if p is None or p is o:
                return
            if (not p.is_dma) and (not o.is_dma) and p.q == q:
                if q == "pe" or kind != "raw":
                    return
            deps.append(p)

        for b in reads:
            add(b.w, "raw")
        for b in writes:
            add(b.w, "waw")
            for r in b.rc.values():
                add(r, "war")
            for r in b.rd:
                add(r, "war")
        if o.is_dma:
            add(dsem.last, "ser")
            dsem.last = o
            dsem.count += 16
            o.sig = dsem.count
        for p in deps:
            p.needs_sig = True
        o.deps = deps
        for b in reads:
            if o.is_dma:
                b.rd.append(o)
            else:
                b.rc[q] = o
        for b in writes:
            b.w = o
            b.rc = {}
            b.rd = []
        self.ops[q].append(o)
        return o

    def barrier(self):
        lasts = []
        for q in CQS:
            for o in reversed(self.ops[q]):
                if (not o.is_dma) and o.fn is not None:
                    lasts.append(o)
                    break
        for s in self.dsems:
            if s.last is not None:
                lasts.append(s.last)
        for q in QS:
            o = Op()
            o.q = q
            o.fn = None
            o.is_dma = False
            o.sem = None
            o.needs_sig = False
            o.sig = None
            o.deps = [p for p in lasts if not (p.q == q and not p.is_dma)]
            for p in o.deps:
                p.needs_sig = True
            self.ops[q].append(o)
        self.dfree = list(self.dsems)
        self.new_epoch()

    def emit(self):
        cnt = {}
        for q in CQS:
            for o in self.ops[q]:
                if o.fn is not None and (not o.is_dma) and o.needs_sig:
                    k = id(o.sem)
                    cnt[k] = cnt.get(k, 0) + 1
                    o.sig = cnt[k]
        ops = self.ops

        def mk(q):
            def body(e):
                known = {}
                for o in ops[q]:
                    for p in o.deps:
                        k = id(p.sem)
                        if known.get(k, 0) < p.sig:
                            e.wait_ge(p.sem.h, p.sig)
                            known[k] = p.sig
                    if o.fn is not None:
                        ins = o.fn(e)
                        if o.needs_sig:
                            ins.then_inc(o.sem.h, 16 if o.is_dma else 1)
            return body

        with self.nc.Block() as block:
            block.sync(mk("sp"))
            block.tensor(mk("pe"))
            block.scalar(mk("act"))
            block.vector(mk("dve"))
            block.gpsimd(mk("pool"))


class TB:
    __slots__ = ("ap", "b")

    def __init__(self, ap, b=None):
        self.ap = ap
        self.b = b if b is not None else Buf()


def build(T, upto=99, debug=False):
    NT = T // 128
    NB = T // 512
    nc = bass.Bass("TRN2", target_bir_lowering=False)

    def din(name, shape):
        return nc.dram_tensor(name, list(shape), F32, kind="ExternalInput").ap()

    x_d = din("x", [T, D])
    wqkv_d = din("w_qkv", [D, 3 * D])
    woa_d = din("w_o_a", [D, D])
    wdkv_d = din("w_dkv", [D, 320])
    wukv_d = din("w_ukv", [256, 2048])
    wdq_d = din("w_dq", [D, 384])
    wuq_d = din("w_uq", [384, 1536])
    wob_d = din("w_o_b", [D, D])
    wup_d = din("w_up", [2, D, DFF])
    wdn_d = din("w_down", [2, DFF, D])
    g_attn_d = [din("g_attn%d" % l, [128, D]) for l in range(2)]
    g_mlp_d = [din("g_mlp%d" % l, [128, D]) for l in range(2)]
    g_kvin_d = din("g_kvin", [128, D])
    g_final_d = din("g_final", [128, D])
    g_kvn_d = din("g_kvn", [128, 256])
    g_q_d = din("g_q", [128, 384])
    g_sub_d = din("g_sub", [128, 1])
    lam_d = din("lam_in", [128, 4, 64])
    ident_d = din("ident", [128, 128])
    tri_d = din("tri", [128, 128])
    cosF_d = din("cosF", [128, T])
    sinF_d = din("sinF", [128, T])
    cosT_d = din("cosT", [128, NT * 32])
    sinT_d = din("sinT", [128, NT * 32])
    kind_s = "ExternalOutput" if debug else "Internal"
    h1_d = nc.dram_tensor("h1", [T, D], F32, kind=kind_s).ap()
    h2_d = nc.dram_tensor("h2", [T, D], F32, kind=kind_s).ap()
    h3_d = nc.dram_tensor("h3", [T, D], F32, kind=kind_s).ap()
    out_d = nc.dram_tensor("out", [T, D], F32, kind="ExternalOutput").ap()

    with ExitStack() as gs:
        kb = KB(nc, gs)

        uniq = [0]

        def sb(st, name, shape, dt):
            uniq[0] += 1
            return st.enter_context(nc.sbuf_tensor("%s_%d" % (name, uniq[0]), list(shape), dt))

        arenaA = sb(gs, "arenaA", [128, 8 * DFF], BF16)
        arenaB = sb(gs, "arenaB", [128, 8 * DFF], BF16)
        ident_bf = TB(sb(gs, "ident_bf", [128, 128], BF16)[:])
        tri_bf = TB(sb(gs, "tri_bf", [128, 128], BF16)[:])
        ones_bf = TB(sb(gs, "ones_bf", [128, 128], BF16)[:])
        ones_f = TB(sb(gs, "ones_f", [128, 128], F32)[:])
        smalls = sb(gs, "smalls", [128, 16], F32)
        nlam = TB(smalls[:, 0:1])
        gsub = TB(smalls[:, 1:2])
        stats_t = sb(gs, "stats", [128, 24], F32)
        stats = [[TB(stats_t[:, 3 * i + j:3 * i + j + 1]) for j in range(3)] for i in range(8)]
        stat_i = [0]
        banks = []
        for i in range(8):
            pt = gs.enter_context(nc.psum_tensor("bank%d" % i, [128, 512], F32))
            banks.append(TB(pt[:]))

        def bf(tb):
            return tb.ap.bitcast(BF16)

        A_ap = arenaA[:]
        B_ap = arenaB[:]

        def dma_in(dst_ap, src_ap, dsem, reads=(), writes=(), q="sp"):
            return kb.op(q, lambda e: e.dma_start(out=dst_ap, in_=src_ap), reads=reads, writes=writes, dsem=dsem)

        def mm(out_ap, lhsT, rhs, start, stop, reads, writes):
            return kb.op("pe", lambda e: e.matmul(out_ap, lhsT, rhs, start=start, stop=stop),
                         reads=reads, writes=writes)

        def act(out_ap, in_ap, func, reads, writes, scale=None, bias=None, accum=None):
            kw = {}
            if scale is not None:
                kw["scale"] = scale
            if bias is not None:
                kw["bias"] = bias
            if accum is not None:
                kw["accum_out"] = accum
            return kb.op("act", lambda e: e.activation(out=out_ap, in_=in_ap, func=func, **kw),
                         reads=reads, writes=writes)

        def tt(q, out_ap, in0, in1, op, reads, writes):
            return kb.op(q, lambda e: e.tensor_tensor(out=out_ap, in0=in0, in1=in1, op=op), reads=reads, writes=writes)

        def stt(out_ap, in0, scalar, in1, op0, op1, reads, writes):
            return kb.op("dve", lambda e: e.scalar_tensor_tensor(out=out_ap, in0=in0, scalar=scalar, in1=in1,
                                                                  op0=op0, op1=op1), reads=reads, writes=writes)

        def cp(q, out_ap, in_ap, reads, writes):
            if q == "act":
                return kb.op("act", lambda e: e.copy(out=out_ap, in_=in_ap), reads=reads, writes=writes)
            return kb.op(q, lambda e: e.tensor_copy(out=out_ap, in_=in_ap), reads=reads, writes=writes)

        def recip(out_ap, in_ap, reads, writes):
            return kb.op("dve", lambda e: e.reciprocal(out=out_ap, in_=in_ap), reads=reads, writes=writes)

        bank_rr = [0]

        def wbank(n=4):
            b = banks[bank_rr[0] % n]
            bank_rr[0] += 1
            return b

        def norm_tile(xt, gB, xn, width, src_ap=None, out_ap=None, eps=EPS):
            src = xt.ap if src_ap is None else src_ap
            dst = xn.ap if out_ap is None else out_ap
            st = stats[stat_i[0] % 8]
            stat_i[0] += 1
            ss, sd, rs = st
            act(dst, src, AF.Square, [xt.b], [xn.b, ss.b], accum=ss.ap)
            act(sd.ap, ss.ap, AF.Sqrt, [ss.b], [sd.b], scale=1.0 / width, bias=eps)
            recip(rs.ap, sd.ap, [sd.b], [rs.b])
            stt(dst, src, rs.ap, gB.ap, ALU.mult, ALU.mult, [xt.b, rs.b, gB.b], [xn.b])

        def transpose_to(xn, n, dst_ap, dst_b, npart=128, rows=128, evq="act"):
            bk = wbank()
            bv = bf(bk)
            for k in range(n):
                o_ap = bv[0:rows, k * 128:(k + 1) * 128]
                i_ap = xn.ap[:, k * rows:(k + 1) * rows]
                kb.op("pe", lambda e, o_ap=o_ap, i_ap=i_ap: e.transpose(o_ap, i_ap, ident_bf.ap),
                      reads=[xn.b, ident_bf.b], writes=[bk.b])
            src = bv[0:rows, 0:n * 128].rearrange("p (k t) -> p k t", k=n)
            cp(evq, dst_ap, src, [bk.b], [dst_b])

        stage_state = {}

        def load_cast(st_list, dst_ap, dst_b, src_ap, shape_free):
            i = stage_state.setdefault(id(st_list), 0)
            stage_state[id(st_list)] = i + 1
            stg, sem = st_list[i % len(st_list)]
            sv = stg.ap
            if shape_free is not None:
                sv = shape_free(stg.ap)
            dma_in(sv, src_ap, sem, writes=[stg.b])
            cp("pool", dst_ap, sv, [stg.b], [dst_b])
            return sv, stg

        with ExitStack() as ps:
            tmpc = TB(sb(ps, "tmpc", [128, 256], F32)[:])
            s0 = kb.dsem()
            dma_in(tmpc.ap[:, 0:128], ident_d, s0, writes=[tmpc.b])
            dma_in(tmpc.ap[:, 128:256], tri_d, s0, writes=[tmpc.b])
            cp("pool", ident_bf.ap, tmpc.ap[:, 0:128], [tmpc.b], [ident_bf.b])
            cp("pool", tri_bf.ap, tmpc.ap[:, 128:256], [tmpc.b], [tri_bf.b])
            kb.op("pool", lambda e: e.memset(ones_bf.ap, 1.0), writes=[ones_bf.b])
            kb.op("pool", lambda e: e.memset(ones_f.ap, 1.0), writes=[ones_f.b])
            lam_t = TB(sb(ps, "lam_t", [128, 4, 64], F32)[:])
            lp = TB(sb(ps, "lp", [128, 2, 64], F32)[:])
            ls = TB(sb(ps, "ls", [128, 4], F32)[:])
            s1 = kb.dsem()
            dma_in(lam_t.ap, lam_d, s1, writes=[lam_t.b])
            dma_in(smalls[:, 1:2], g_sub_d, s1, writes=[gsub.b])
            tt("dve", lp.ap[:, 0, :], lam_t.ap[:, 0, :], lam_t.ap[:, 1, :], ALU.mult, [lam_t.b], [lp.b])
            tt("dve", lp.ap[:, 1, :], lam_t.ap[:, 2, :], lam_t.ap[:, 3, :], ALU.mult, [lam_t.b, lp.b], [lp.b])
            kb.op("dve", lambda e: e.reduce_sum(out=ls.ap[:, 0:2], in_=lp.ap, axis=mybir.AxisListType.X),
                  reads=[lp.b], writes=[ls.b])
            act(ls.ap[:, 2:4], ls.ap[:, 0:2], AF.Exp, [ls.b], [ls.b])
            kb.op("dve", lambda e: e.scalar_tensor_tensor(out=nlam.ap, in0=ls.ap[:, 3:4], scalar=-LAMBDA_INIT,
                                                          in1=ls.ap[:, 2:3], op0=ALU.add, op1=ALU.subtract),
                  reads=[ls.b], writes=[nlam.b])
            kb.op("dve", lambda e: e.tensor_scalar(out=gsub.ap, in0=gsub.ap, scalar1=1.0 - LAMBDA_INIT, scalar2=None,
                                                   op0=ALU.mult), reads=[gsub.b], writes=[gsub.b])
            kb.barrier()

        def pre_pass_simple(src_d, g_d, dstT):
            with ExitStack() as ps:
                xts = [(TB(sb(ps, "xt%d" % i, [128, D], F32)[:]), kb.dsem()) for i in range(5)]
                xns = [TB(sb(ps, "xn%d" % i, [128, D], BF16)[:]) for i in range(3)]
                gB = TB(sb(ps, "gB", [128, D], F32)[:])
                sg = kb.dsem()
                dma_in(gB.ap, g_d, sg, writes=[gB.b])
                tb = [Buf() for _ in range(NT)]

                def ld(i):
                    xt, sem = xts[i % 5]
                    dma_in(xt.ap, src_d[i * 128:(i + 1) * 128, :], sem, writes=[xt.b])
                for i in range(min(4, NT)):
                    ld(i)
                for i in range(NT):
                    if i + 4 < NT:
                        ld(i + 4)
                    xt = xts[i % 5][0]
                    xn = xns[i % 3]
                    norm_tile(xt, gB, xn, D)
                    transpose_to(xn, 8, dstT[:, :, i * 128:(i + 1) * 128], tb[i], evq=("act" if i % 2 else "dve"))
                kb.barrier()
            return tb

        def wo_pass(src_d, w_d, OT, dst_d, prefetch=None):
            with ExitStack() as ps:
                xts = [(TB(sb(ps, "xt%d" % i, [128, D], F32)[:]), kb.dsem(), kb.dsem()) for i in range(3)]
                stg = [(TB(sb(ps, "wstg%d" % i, [128, D], F32)[:]), kb.dsem()) for i in range(2)]
                wo = TB(sb(ps, "wo_b", [128, 8, D], BF16)[:])
                for h in range(8):
                    load_cast(stg, wo.ap[:, h, :], wo.b, w_d[h * 128:(h + 1) * 128, :], None)

                def ld(i):
                    xt, sem, _ = xts[i % 3]
                    dma_in(xt.ap, src_d[i * 128:(i + 1) * 128, :], sem, writes=[xt.b])
                ld(0)
                if NT > 1:
                    ld(1)
                for i in range(NT):
                    if i + 2 < NT:
                        ld(i + 2)
                    xt, _, ssem = xts[i % 3]
                    for half in range(2):
                        bk = wbank(8)
                        for h in range(8):
                            mm(bk.ap, OT[:, h, i * 128:(i + 1) * 128], wo.ap[:, h, half * 512:(half + 1) * 512],
                               h == 0, h == 7, [wo.b], [bk.b])
                        tt("dve", xt.ap[:, half * 512:(half + 1) * 512], bk.ap, xt.ap[:, half * 512:(half + 1) * 512],
                           ALU.add, [bk.b, xt.b], [xt.b])
                    dma_in(dst_d[i * 128:(i + 1) * 128, :], xt.ap, ssem, reads=[xt.b], q="pool")
                    if prefetch is not None:
                        prefetch(i, stg)
                kb.barrier()

        def mlp_pass(src_d, l, g_d, dst_d, wup_loaded, final_g=None):
            with ExitStack() as ps:
                Wup = A_ap[:, 0:8 * DFF].rearrange("p (k f) -> p k f", k=8)
                Wdn = B_ap[:, 0:32 * D].rearrange("p (k f) -> p k f", k=32)
                wupB = wup_loaded if wup_loaded is not None else Buf()
                wdnB = [Buf() for _ in range(32)]
                hts = [(TB(sb(ps, "ht%d" % i, [128, D], F32)[:]), kb.dsem(), kb.dsem()) for i in range(4)]
                xns = [TB(sb(ps, "xn%d" % i, [128, D], BF16)[:]) for i in range(2)]
                hnT = sb(ps, "hnTb", [128, 8, 512], BF16)[:]
                hnB = [Buf() for _ in range(4)]
                aT = sb(ps, "aT", [128, 32, 512], BF16)[:]
                aB = [Buf() for _ in range(32)]
                rts = [TB(sb(ps, "rt%d" % i, [128, 512], BF16)[:]) for i in range(2)]
                gB = TB(sb(ps, "gB", [128, D], F32)[:])
                stg = [(TB(sb(ps, "wstg%d" % i, [128, 512], F32)[:]), kb.dsem()) for i in range(2)]
                sg = kb.dsem()
                dma_in(gB.ap, g_d, sg, writes=[gB.b])
                gF = None
                if final_g is not None:
                    gF = TB(sb(ps, "gF", [128, D], F32)[:])
                    dma_in(gF.ap, final_g, sg, writes=[gF.b])

                def ld_block(b):
                    for j in range(4):
                        ht, sem, _ = hts[j]
                        i = 4 * b + j
                        dma_in(ht.ap, src_d[i * 128:(i + 1) * 128, :], sem, writes=[ht.b])
                ld_block(0)
                if wup_loaded is None:
                    for dk in range(8):
                        for qq in range(8):
                            load_cast(stg, Wup[:, dk, qq * 512:(qq + 1) * 512], wupB,
                                      wup_d[l, dk * 128:(dk + 1) * 128, qq * 512:(qq + 1) * 512], None)
                for f in range(32):
                    for hf in range(2):
                        load_cast(stg, Wdn[:, f, hf * 512:(hf + 1) * 512], wdnB[f],
                                  wdn_d[l, f * 128:(f + 1) * 128, hf * 512:(hf + 1) * 512], None)
                for b in range(NB):
                    if b > 0:
                        ld_block(b)
                    for j in range(4):
                        ht = hts[j][0]
                        xn = xns[j % 2]
                        norm_tile(ht, gB, xn, D)
                        transpose_to(xn, 8, hnT[:, :, j * 128:(j + 1) * 128], hnB[j], evq=("act" if j % 2 else "dve"))
                    for f in range(32):
                        bk = wbank(4)
                        for dk in range(8):
                            mm(bk.ap, Wup[:, dk, f * 128:(f + 1) * 128], hnT[:, dk, :], dk == 0, dk == 7,
                               [wupB] + hnB, [bk.b])
                        rt = rts[f % 2]
                        act(rt.ap, bk.ap, AF.Relu, [bk.b], [rt.b])
                        tt("dve", aT[:, f, :], rt.ap, rt.ap, ALU.mult, [rt.b], [aB[f]])
                    for j in range(4):
                        ht, _, ssem = hts[j]
                        for half in range(2):
                            bk = banks[4 + (2 * j + half) % 4]
                            for f in range(32):
                                mm(bk.ap, aT[:, f, j * 128:(j + 1) * 128], Wdn[:, f, half * 512:(half + 1) * 512],
                                   f == 0, f == 31, [aB[f], wdnB[f]], [bk.b])
                            tt("dve", ht.ap[:, half * 512:(half + 1) * 512], bk.ap,
                               ht.ap[:, half * 512:(half + 1) * 512], ALU.add, [bk.b, ht.b], [ht.b])
                        i = 4 * b + j
                        if gF is None:
                            dma_in(dst_d[i * 128:(i + 1) * 128, :], ht.ap, ssem, reads=[ht.b], q="pool")
                        else:
                            st = stats[stat_i[0] % 8]
                            stat_i[0] += 1
                            ss, sd, rs = st
                            xn = xns[j % 2]
                            act(xn.ap, ht.ap, AF.Square, [ht.b], [xn.b, ss.b], accum=ss.ap)
                            act(sd.ap, ss.ap, AF.Sqrt, [ss.b], [sd.b], scale=1.0 / D, bias=EPS)
                            recip(rs.ap, sd.ap, [sd.b], [rs.b])
                            stt(ht.ap, ht.ap, rs.ap, gF.ap, ALU.mult, ALU.mult, [ht.b, rs.b, gF.b], [ht.b])
                            dma_in(dst_d[i * 128:(i + 1) * 128, :], ht.ap, ssem, reads=[ht.b], q="pool")
                kb.barrier()

        def attention_block(qb, comps, Vh, vB, E, scale, pending):
            nch = 4 * qb + 4
            q0 = qb * 512

            def cols_of(kc):
                j = kc - 4 * qb
                return (128 * j if j > 0 else 0), j

            def qk(kc):
                lo, j = cols_of(kc)
                for c, cm in enumerate(comps):
                    S = cm["S"][kc % 2]
                    parts = cm["parts"]
                    for pi, (KT, QT, bufs) in enumerate(parts):
                        mm(S.ap[:, lo:512], KT[:, kc * 128:(kc + 1) * 128], QT[:, q0 + lo:q0 + 512],
                           pi == 0, (pi == len(parts) - 1) and j < 0, bufs, [S.b])
                    if j >= 0:
                        mm(S.ap[:, lo:lo + 128], ident_bf.ap, tri_bf.ap, False, True,
                           [ident_bf.b, tri_bf.b], [S.b])
                    e = E[c][kc % 3]
                    act(e.ap[:, lo:512], S.ap[:, lo:512], AF.Exp, [S.b], [e.b], scale=scale)

            def pv(kc):
                lo, j = cols_of(kc)
                for c, cm in enumerate(comps):
                    e = E[c][kc % 3]
                    mm(cm["O"].ap[:, lo:512], Vh[:, kc, :], e.ap[:, lo:512], kc == 0, kc == nch - 1,
                       [e.b, vB], [cm["O"].b])
                    mm(cm["L"].ap[:, lo:512], ones_bf.ap, e.ap[:, lo:512], kc == 0, kc == nch - 1,
                       [e.b, ones_bf.b], [cm["L"].b])

            qk(0)
            for kc in range(nch):
                if kc + 1 < nch:
                    qk(kc + 1)
                pv(kc)
                if kc == nch - 2 or nch == 1:
                    while pending:
                        pending.pop(0)()

        hnT0 = A_ap[:, 0:8 * T].rearrange("p (k t) -> p k t", k=8)
        OnT = B_ap[:, 0:8 * T].rearrange("p (k t) -> p k t", k=8)
        if upto >= 1:
            hnB0 = pre_pass_simple(x_d, g_attn_d[0], hnT0)
        if upto >= 2:
            with ExitStack() as ps:
                stg = [(TB(sb(ps, "wstg%d" % i, [128, 8, 128], F32)[:]), kb.dsem()) for i in range(2)]
                wb = TB(sb(ps, "wb", [128, 5, 8, 128], BF16)[:])
                QT = TB(sb(ps, "QT", [128, T], BF16)[:])
                KT = TB(sb(ps, "KT", [128, T], BF16)[:])
                Vh = TB(sb(ps, "Vh", [128, NT, 128], BF16)[:])
                css = [(TB(sb(ps, "cs%d" % i, [128, 2, 512], F32)[:]), kb.dsem()) for i in range(2)]
                tA = TB(sb(ps, "tA", [128, 512], F32)[:])
                tB_ = TB(sb(ps, "tB", [128, 512], F32)[:])
                E = [[TB(sb(ps, "E%d_%d" % (c, i), [128, 512], BF16)[:]) for i in range(3)] for c in range(2)]
                cL = [TB(sb(ps, "cL%d" % c, [128, 512], F32)[:]) for c in range(2)]
                cO = [TB(sb(ps, "cO%d" % c, [128, 512], F32)[:]) for c in range(2)]
                sq = TB(sb(ps, "sq", [128, 512], F32)[:])
                onB = [[Buf() for _ in range(NB)] for _ in range(8)]
                pending = []

                def load_w(h):
                    for s in range(3):
                        src = wqkv_d[:, s * 1024 + h * 128:s * 1024 + (h + 1) * 128].rearrange(
                            "(k p) j -> p k j", p=128)
                        sv, st = load_cast(stg, wb.ap[:, s, :, :], wb.b, src, None)
                        if s < 2:
                            s6 = sv.rearrange("p k (c h d) -> p k c h d", c=2, h=2)
                            d6 = wb.ap[:, 3 + s, :, :].rearrange("p k (c h d) -> p k c h d", c=2, h=2)
                            for c in range(2):
                                kb.op("pool", lambda e, o=d6[:, :, c, 0, :], i=s6[:, :, c, 1, :]:
                                      e.tensor_scalar(out=o, in0=i, scalar1=-1.0, scalar2=0.0, op0=ALU.mult,
                                                      op1=ALU.add), reads=[st.b], writes=[wb.b])
                                cp("pool", d6[:, :, c, 1, :], s6[:, :, c, 0, :], [st.b], [wb.b])

                def ld_cs(b):
                    cs, sem = css[b % 2]
                    dma_in(cs.ap[:, 0, :], cosF_d[:, b * 512:(b + 1) * 512], sem, writes=[cs.b])
                    dma_in(cs.ap[:, 1, :], sinF_d[:, b * 512:(b + 1) * 512], sem, writes=[cs.b])

                def proj(h):
                    ld_cs(0)
                    for b in range(NB):
                        if b + 1 < NB:
                            ld_cs(b + 1)
                        cs = css[b % 2][0]
                        hb = hnB0[4 * b:4 * b + 4]
                        for (wi, dst) in ((0, QT), (1, KT)):
                            bA = wbank(4)
                            bB = wbank(4)
                            for dk in range(8):
                                mm(bA.ap, wb.ap[:, wi, dk, :], hnT0[:, dk, b * 512:(b + 1) * 512], dk == 0, dk == 7,
                                   [wb.b] + hb, [bA.b])
                            for dk in range(8):
                                mm(bB.ap, wb.ap[:, 3 + wi, dk, :], hnT0[:, dk, b * 512:(b + 1) * 512], dk == 0,
                                   dk == 7, [wb.b] + hb, [bB.b])
                            tt("dve", tA.ap, bA.ap, cs.ap[:, 0, :], ALU.mult, [bA.b, cs.b], [tA.b])
                            tt("dve", tB_.ap, bB.ap, cs.ap[:, 1, :], ALU.mult, [bB.b, cs.b], [tB_.b])
                            tt("dve", dst.ap[:, b * 512:(b + 1) * 512], tA.ap, tB_.ap, ALU.add, [tA.b, tB_.b], [dst.b])
                        bV = wbank(4)
                        for j in range(4):
                            for dk in range(8):
                                mm(bV.ap[:, j * 128:(j + 1) * 128], hnT0[:, dk, (4 * b + j) * 128:(4 * b + j + 1) * 128],
                                   wb.ap[:, 2, dk, :], dk == 0, dk == 7, [wb.b] + hb, [bV.b])
                        cp("act", Vh.ap[:, 4 * b:4 * b + 4, :], bV.ap.rearrange("p (j e) -> p j e", j=4), [bV.b], [Vh.b])

                def epilogue(h, qb, comps):
                    cp("dve", cO[0].ap, comps[0]["O"].ap, [comps[0]["O"].b], [cO[0].b])
                    cp("dve", cO[1].ap, comps[1]["O"].ap, [comps[1]["O"].b], [cO[1].b])
                    for c in range(2):
                        recip(cL[c].ap, comps[c]["L"].ap, [comps[c]["L"].b], [cL[c].b])
                    for c in range(2):
                        tt("dve", cO[c].ap, cO[c].ap, cL[c].ap, ALU.mult, [cO[c].b, cL[c].b], [cO[c].b])
                    stt(cO[0].ap, cO[1].ap, nlam.ap, cO[0].ap, ALU.mult, ALU.add, [cO[0].b, cO[1].b, nlam.b], [cO[0].b])
                    tt("dve", sq.ap, cO[0].ap, cO[0].ap, ALU.mult, [cO[0].b], [sq.b])

                    def tail():
                        bk = banks[0]
                        mm(bk.ap, ones_f.ap, sq.ap, True, True, [ones_f.b, sq.b], [bk.b])
                        act(cL[0].ap, bk.ap, AF.Ln, [bk.b], [cL[0].b], scale=1.0 / 128, bias=SUBLN_EPS)
                        act(cL[0].ap, cL[0].ap, AF.Exp, [cL[0].b], [cL[0].b], scale=-0.5)
                        stt(OnT[:, h, qb * 512:(qb + 1) * 512], cO[0].ap, gsub.ap, cL[0].ap, ALU.mult, ALU.mult,
                            [cO[0].b, gsub.b, cL[0].b], [onB[h][qb]])
                    pending.append(tail)

                load_w(0)
                for h in range(8):
                    proj(h)
                    while pending:
                        pending.pop(0)()
                    if h + 1 < 8:
                        load_w(h + 1)
                    comps = []
                    for c in range(2):
                        comps.append({
                            "parts": [(KT.ap[c * 64:(c + 1) * 64, :], QT.ap[c * 64:(c + 1) * 64, :], [KT.b, QT.b])],
                            "S": [banks[2 * c], banks[2 * c + 1]], "O": banks[4 + c], "L": banks[6 + c]})
                    for qb in range(NB):
                        attention_block(qb, comps, Vh.ap, Vh.b, E, 0.125, pending)
                        epilogue(h, qb, comps)
                while pending:
                    pending.pop(0)()
                kb.barrier()
        wupB0 = None
        if upto >= 3:
            Wup = A_ap[:, 0:8 * DFF].rearrange("p (k f) -> p k f", k=8)
            wupB0 = Buf()

            def pf0(i, stg):
                per = (32 + NT - 1) // NT
                for pi in range(i * per, min(32, (i + 1) * per)):
                    dk, qq = pi // 4, pi % 4
                    load_cast(stg, Wup[:, dk, qq * 1024:(qq + 1) * 1024], wupB0,
                              wup_d[0, dk * 128:(dk + 1) * 128, qq * 1024:(qq + 1) * 1024], None)
            wo_pass(x_d, woa_d, OnT, h1_d, prefetch=pf0)
        if upto >= 4:
            mlp_pass(h1_d, 0, g_mlp_d[0], h2_d, wupB0)

        cqnT = A_ap[:, 0:3 * T].rearrange("p (k t) -> p k t", k=3)
        cnT = A_ap[:, 3 * T:5 * T].rearrange("p (k t) -> p k t", k=2)
        krT = A_ap[:, 5 * T:6 * T]
        OT1 = B_ap[:, 0:8 * T].rearrange("p (k t) -> p k t", k=8)
        if upto >= 5:
            with ExitStack() as ps:
                xts = [(TB(sb(ps, "xt%d" % i, [128, D], F32)[:]), kb.dsem()) for i in range(4)]
                xns = [TB(sb(ps, "xn%d" % i, [128, D], BF16)[:]) for i in range(4)]
                gKV = TB(sb(ps, "gKV", [128, D], F32)[:])
                gA1 = TB(sb(ps, "gA1", [128, D], F32)[:])
                gkn = TB(sb(ps, "gkn", [128, 256], F32)[:])
                gq = TB(sb(ps, "gq", [128, 384], F32)[:])
                csT = TB(sb(ps, "csT", [128, 2, NT, 32], F32)[:])
                stg = [(TB(sb(ps, "wstg%d" % i, [128, 8, 384], F32)[:]), kb.dsem()) for i in range(1)]
                wdkv = TB(sb(ps, "wdkv", [128, 8, 320], BF16)[:])
                wdq = TB(sb(ps, "wdq", [128, 8, 384], BF16)[:])
                hTs = [TB(sb(ps, "hT%d" % i, [128, 8, 128], BF16)[:]) for i in range(4)]
                cnbs = [TB(sb(ps, "cnb%d" % i, [128, 384], BF16)[:]) for i in range(4)]
                krbs = [TB(sb(ps, "krb%d" % i, [128, 64], BF16)[:]) for i in range(2)]
                rtmps = [TB(sb(ps, "rtmp%d" % i, [128, 4, 32], F32)[:]) for i in range(2)]
                sg = kb.dsem()
                dma_in(gKV.ap, g_kvin_d, sg, writes=[gKV.b])
                dma_in(gA1.ap, g_attn_d[1], sg, writes=[gA1.b])
                dma_in(gkn.ap, g_kvn_d, sg, writes=[gkn.b])
                dma_in(gq.ap, g_q_d, sg, writes=[gq.b])
                dma_in(csT.ap[:, 0, :, :], cosT_d.rearrange("p (n f) -> p n f", f=32), sg, writes=[csT.b])
                dma_in(csT.ap[:, 1, :, :], sinT_d.rearrange("p (n f) -> p n f", f=32), sg, writes=[csT.b])
                load_cast(stg, wdkv.ap, wdkv.b, wdkv_d.rearrange("(k p) j -> p k j", p=128), lambda a: a[:, :, 0:320])
                load_cast(stg, wdq.ap, wdq.b, wdq_d.rearrange("(k p) j -> p k j", p=128), None)
                preB = [Buf() for _ in range(NT)]

                def ld(i):
                    xt, sem = xts[i % 4]
                    dma_in(xt.ap, h2_d[i * 128:(i + 1) * 128, :], sem, writes=[xt.b])
                for i in range(min(3, NT)):
                    ld(i)
                for i in range(NT):
                    if i + 3 < NT:
                        ld(i + 3)
                    xt = xts[i % 4][0]
                    cnb = cnbs[i % 2]
                    krb = krbs[i % 2]
                    rtmp = rtmps[i % 2]
                    xn = xns[i % 2]
                    norm_tile(xt, gKV, xn, D)
                    hT = hTs[i % 2]
                    transpose_to(xn, 8, hT.ap, hT.b, evq="act")
                    bk = wbank(8)
                    for dk in range(8):
                        mm(bk.ap[:, 0:320], hT.ap[:, dk, :], wdkv.ap[:, dk, :], dk == 0, dk == 7, [hT.b, wdkv.b], [bk.b])
                    norm_tile(bk, gkn, cnb, 256, src_ap=bk.ap[:, 0:256], out_ap=cnb.ap[:, 0:256])
                    x1 = bk.ap[:, 256:288]
                    x2 = bk.ap[:, 288:320]
                    cc = csT.ap[:, 0, i, :]
                    sn = csT.ap[:, 1, i, :]
                    tt("dve", rtmp.ap[:, 0, :], x1, cc, ALU.mult, [bk.b, csT.b], [rtmp.b])
                    tt("dve", rtmp.ap[:, 1, :], x2, sn, ALU.mult, [bk.b, csT.b, rtmp.b], [rtmp.b])
                    tt("dve", rtmp.ap[:, 2, :], x2, cc, ALU.mult, [bk.b, csT.b, rtmp.b], [rtmp.b])
                    tt("dve", rtmp.ap[:, 3, :], x1, sn, ALU.mult, [bk.b, csT.b, rtmp.b], [rtmp.b])
                    tt("dve", krb.ap[:, 0:32], rtmp.ap[:, 0, :], rtmp.ap[:, 1, :], ALU.subtract, [rtmp.b], [krb.b])
                    tt("dve", krb.ap[:, 32:64], rtmp.ap[:, 2, :], rtmp.ap[:, 3, :], ALU.add, [rtmp.b, krb.b], [krb.b])
                    b2 = wbank(8)
                    bv = bf(b2)
                    for k in range(2):
                        kb.op("pe", lambda e, o=bv[:, k * 128:(k + 1) * 128], a=cnb.ap[:, k * 128:(k + 1) * 128]:
                              e.transpose(o, a, ident_bf.ap), reads=[cnb.b, ident_bf.b], writes=[b2.b])
                    kb.op("pe", lambda e, o=bv[0:64, 256:384], a=krb.ap: e.transpose(o, a, ident_bf.ap),
                          reads=[krb.b, ident_bf.b], writes=[b2.b])
                    cp("dve", cnT[:, :, i * 128:(i + 1) * 128], bv[:, 0:256].rearrange("p (k t) -> p k t", k=2),
                       [b2.b], [preB[i]])
                    cp("dve", krT[0:64, i * 128:(i + 1) * 128], bv[0:64, 256:384], [b2.b], [preB[i]])
                    xn = xns[2 + i % 2]
                    cnb = cnbs[2 + i % 2]
                    norm_tile(xt, gA1, xn, D)
                    hT = hTs[2 + i % 2]
                    transpose_to(xn, 8, hT.ap, hT.b, evq="act")
                    bk = wbank(8)
                    for dk in range(8):
                        mm(bk.ap[:, 0:384], hT.ap[:, dk, :], wdq.ap[:, dk, :], dk == 0, dk == 7, [hT.b, wdq.b], [bk.b])
                    norm_tile(bk, gq, cnb, 384, src_ap=bk.ap[:, 0:384])
                    b2 = wbank(8)
                    bv = bf(b2)
                    for k in range(3):
                        kb.op("pe", lambda e, o=bv[:, k * 128:(k + 1) * 128], a=cnb.ap[:, k * 128:(k + 1) * 128]:
                              e.transpose(o, a, ident_bf.ap), reads=[cnb.b, ident_bf.b], writes=[b2.b])
                    cp("dve", cqnT[:, :, i * 128:(i + 1) * 128], bv[:, 0:384].rearrange("p (k t) -> p k t", k=3),
                       [b2.b], [preB[i]])
                kb.barrier()
        if upto >= 6:
            with ExitStack() as ps:
                stq = [(TB(sb(ps, "stq", [128, 3, 192], F32)[:]), kb.dsem())]
                stkv = [(TB(sb(ps, "stkv", [128, 2, 256], F32)[:]), kb.dsem())]
                wq = TB(sb(ps, "wq", [128, 3, 256], BF16)[:])
                wkv = TB(sb(ps, "wkv", [128, 2, 256], BF16)[:])
                qnT = TB(sb(ps, "qnT", [128, T], BF16)[:])
                knT = TB(sb(ps, "knT", [128, T], BF16)[:])
                qrT = TB(sb(ps, "qrT", [128, T], BF16)[:])
                Vh = TB(sb(ps, "Vh", [128, NT, 128], BF16)[:])
                css = [(TB(sb(ps, "cs%d" % i, [128, 2, 512], F32)[:]), kb.dsem()) for i in range(2)]
                tA = TB(sb(ps, "tA", [128, 512], F32)[:])
                tB_ = TB(sb(ps, "tB", [128, 512], F32)[:])
                E = [[TB(sb(ps, "E%d" % i, [128, 512], BF16)[:]) for i in range(3)]]
                rL = TB(sb(ps, "rL", [128, 512], F32)[:])
                oB = [[Buf() for _ in range(NB)] for _ in range(8)]
                pending = []
                allpre = preB
                krB = Buf()

                def load_w1(h):
                    sv, st = load_cast(stq, wq.ap[:, :, 0:192], wq.b,
                                       wuq_d[:, h * 192:(h + 1) * 192].rearrange("(k p) j -> p k j", p=128), None)
                    kb.op("pool", lambda e, o=wq.ap[:, :, 192:224], i=sv[:, :, 160:192]:
                          e.tensor_scalar(out=o, in0=i, scalar1=-1.0, scalar2=0.0, op0=ALU.mult, op1=ALU.add),
                          reads=[st.b], writes=[wq.b])
                    cp("pool", wq.ap[:, :, 224:256], sv[:, :, 128:160], [st.b], [wq.b])
                    load_cast(stkv, wkv.ap, wkv.b,
                              wukv_d[:, h * 256:(h + 1) * 256].rearrange("(k p) j -> p k j", p=128), None)

                def ld_cs(b):
                    cs, sem = css[b % 2]
                    dma_in(cs.ap[0:64, 0, :], cosF_d[0:64, b * 512:(b + 1) * 512], sem, writes=[cs.b])
                    dma_in(cs.ap[0:64, 1, :], sinF_d[0:64, b * 512:(b + 1) * 512], sem, writes=[cs.b])

                def proj1(h):
                    ld_cs(0)
                    for b in range(NB):
                        if b + 1 < NB:
                            ld_cs(b + 1)
                        cs = css[b % 2][0]
                        pb = allpre[4 * b:4 * b + 4]
                        blk = slice(b * 512, (b + 1) * 512)
                        bk = wbank(4)
                        for k in range(3):
                            mm(bk.ap, wq.ap[:, k, 0:128], cqnT[:, k, blk], k == 0, k == 2, [wq.b] + pb, [bk.b])
                        cp("act", qnT.ap[:, blk], bk.ap, [bk.b], [qnT.b])
                        bA = wbank(4)
                        bB = wbank(4)
                        for k in range(3):
                            mm(bA.ap[0:64, :], wq.ap[:, k, 128:192], cqnT[:, k, blk], k == 0, k == 2, [wq.b] + pb, [bA.b])
                        for k in range(3):
                            mm(bB.ap[0:64, :], wq.ap[:, k, 192:256], cqnT[:, k, blk], k == 0, k == 2, [wq.b] + pb, [bB.b])
                        tt("dve", tA.ap[0:64, :], bA.ap[0:64, :], cs.ap[0:64, 0, :], ALU.mult, [bA.b, cs.b], [tA.b])
                        tt("dve", tB_.ap[0:64, :], bB.ap[0:64, :], cs.ap[0:64, 1, :], ALU.mult, [bB.b, cs.b], [tB_.b])
                        tt("dve", qrT.ap[0:64, blk], tA.ap[0:64, :], tB_.ap[0:64, :], ALU.add, [tA.b, tB_.b], [qrT.b])
                        bk = wbank(4)
                        for k in range(2):
                            mm(bk.ap, wkv.ap[:, k, 0:128], cnT[:, k, blk], k == 0, k == 1, [wkv.b] + pb, [bk.b])
                        cp("act", knT.ap[:, blk], bk.ap, [bk.b], [knT.b])
                        bV = wbank(4)
                        for j in range(4):
                            tl = slice((4 * b + j) * 128, (4 * b + j + 1) * 128)
                            for k in range(2):
                                mm(bV.ap[:, j * 128:(j + 1) * 128], cnT[:, k, tl], wkv.ap[:, k, 128:256], k == 0, k == 1,
                                   [wkv.b] + pb, [bV.b])
                        cp("act", Vh.ap[:, 4 * b:4 * b + 4, :], bV.ap.rearrange("p (j e) -> p j e", j=4), [bV.b], [Vh.b])

                load_w1(0)
                sc1 = 192.0 ** -0.5
                for h in range(8):
                    proj1(h)
                    if h + 1 < 8:
                        load_w1(h + 1)
                    for qb in range(NB):
                        acc = 4 + 2 * (qb % 2)
                        comps = [{
                            "parts": [(knT.ap, qnT.ap, [knT.b, qnT.b]),
                                      (krT[0:64, :], qrT.ap[0:64, :], [qrT.b] + allpre)],
                            "S": [banks[0], banks[1]], "O": banks[acc], "L": banks[acc + 1]}]
                        attention_block(qb, comps, Vh.ap, Vh.b, E, sc1, pending)
                        cm = comps[0]
                        recip(rL.ap, cm["L"].ap, [cm["L"].b], [rL.b])
                        tt("dve", OT1[:, h, qb * 512:(qb + 1) * 512], cm["O"].ap, rL.ap, ALU.mult,
                           [cm["O"].b, rL.b], [oB[h][qb]])
                kb.barrier()
        wupB1 = None
        if upto >= 7:
            Wup = A_ap[:, 0:8 * DFF].rearrange("p (k f) -> p k f", k=8)
            wupB1 = Buf()

            def pf1(i, stg):
                per = (32 + NT - 1) // NT
                for pi in range(i * per, min(32, (i + 1) * per)):
                    dk, qq = pi // 4, pi % 4
                    load_cast(stg, Wup[:, dk, qq * 1024:(qq + 1) * 1024], wupB1,
                              wup_d[1, dk * 128:(dk + 1) * 128, qq * 1024:(qq + 1) * 1024], None)
            wo_pass(h2_d, wob_d, OT1, h3_d, prefetch=pf1)
        if upto >= 8:
            mlp_pass(h3_d, 1, g_mlp_d[1], out_d, wupB1, final_g=g_final_d)
        kb.barrier()
        kb.emit()
    return nc


def _consts(T):
    ident = np.eye(128, dtype=np.float32)
    p = np.arange(128)[:, None]
    f = np.arange(128)[None, :]
    tri = np.where(f >= p, 0.0, NEG).astype(np.float32)
    pos = np.arange(T, dtype=np.float32)
    inv = (np.float32(10000.0) ** (-(np.arange(0, 64, 2, dtype=np.float32)) / np.float32(64))).astype(np.float32)
    ang = (pos[:, None] * inv[None, :]).astype(np.float32)
    cosT = np.cos(ang).astype(np.float32)
    sinT = np.sin(ang).astype(np.float32)
    idx = np.arange(128) % 32
    cosF = np.ascontiguousarray(cosT[:, idx].T)
    sinF = np.ascontiguousarray(sinT[:, idx].T)
    NT = T // 128
    cosL = np.ascontiguousarray(cosT.reshape(NT, 128, 32).transpose(1, 0, 2).reshape(128, NT * 32))
    sinL = np.ascontiguousarray(sinT.reshape(NT, 128, 32).transpose(1, 0, 2).reshape(128, NT * 32))
    return dict(ident=ident, tri=tri, cosF=cosF, sinF=sinF, cosT=cosL, sinT=sinL)


def _rep(v, n=128):
    v = np.asarray(v, dtype=np.float32).reshape(1, -1)
    return np.ascontiguousarray(np.broadcast_to(v, (n, v.shape[1])))


def make_in_maps(inp, T, ncores):
    c = _consts(T)
    f = lambda a: np.ascontiguousarray(np.asarray(a, dtype=np.float32))
    shared = dict(
        w_qkv=f(inp["w_qkv_a"][0]), w_o_a=f(inp["w_o_a"][0]), w_dkv=f(inp["w_dkv"]), w_ukv=f(inp["w_ukv"]),
        w_dq=f(inp["w_dq"][0]), w_uq=f(inp["w_uq"][0]), w_o_b=f(inp["w_o_b"][0]),
        w_up=f(inp["w_up"]), w_down=f(inp["w_down"]),
        g_attn0=_rep(inp["attn_norm_g"][0]), g_attn1=_rep(inp["attn_norm_g"][1]),
        g_mlp0=_rep(inp["mlp_norm_g"][0]), g_mlp1=_rep(inp["mlp_norm_g"][1]),
        g_kvin=_rep(inp["kv_in_norm_g"]), g_final=_rep(inp["final_norm_g"]),
        g_kvn=_rep(inp["kv_norm_g"]), g_q=_rep(inp["q_norm_g"][0]),
        g_sub=np.ascontiguousarray(np.asarray(inp["subln_g"][0], dtype=np.float32).reshape(128, 1)),
        lam_in=np.ascontiguousarray(np.stack([_rep(inp["lambda_q1"][0]), _rep(inp["lambda_k1"][0]),
                                              _rep(inp["lambda_q2"][0]), _rep(inp["lambda_k2"][0])], axis=1)),
        **c)
    x = np.asarray(inp["x"], dtype=np.float32)
    maps = []
    for b in range(ncores):
        m = dict(shared)
        m["x"] = np.ascontiguousarray(x[b, :T])
        maps.append(m)
    return maps


def kernel(**inputs):
    T = 4096
    nc = build(T)
    maps = make_in_maps(inputs, T, 8)
    res = run_bass_kernel_spmd(nc, maps, core_ids=list(range(8)))
    out = np.stack([np.asarray(r["out"], dtype=np.float32).reshape(T, D) for r in res.results], axis=0)
    return out
```
